# Optimizing a Trainium2 kernel written in Bass

```python
import jax, jax.numpy as jnp
from jax import lax
import numpy as np

D_MODEL = 1024
BATCH = 8
SEQ = 4096
DEPTH = 2

EXPAND = 2
WIDTH = EXPAND * D_MODEL
HEAD_DIM = 128
A_HEADS = WIDTH // HEAD_DIM
B_HEADS = WIDTH // HEAD_DIM
N_A = DEPTH // 2
N_B = DEPTH - N_A
CHUNK = 64
Q_BLOCK = 128
EPS = 1e-6

kernel_name = "yoco_hgrn2_fox_hybrid"


def _rms(x, g):
    xf = x.astype(jnp.float32)
    y = xf * lax.rsqrt(jnp.mean(xf * xf, axis=-1, keepdims=True) + EPS)
    return (y * g.astype(jnp.float32)).astype(x.dtype)


def _ada(c, w, b):
    return jax.nn.silu(c) @ w + b


def _hgrn2(h, w_in, lb, onorm_g, w_out):
    bsz, s, _ = h.shape
    f32 = jnp.float32
    q, fz, i, g = jnp.split(h @ w_in, 4, axis=-1)
    fz = fz.astype(f32)
    logf = jnp.log(lb + (1.0 - lb) * jax.nn.sigmoid(fz))
    k = (1.0 - lb) * jax.nn.sigmoid(-fz)

    def hc(t):
        return t.reshape(bsz, s // CHUNK, CHUNK, A_HEADS, HEAD_DIM).transpose(0, 3, 1, 2, 4)

    q, k, v, logf = hc(q.astype(f32)), hc(k), hc(i.astype(f32)), hc(logf)
    b = jnp.cumsum(logf, axis=3)
    b_last = b[:, :, :, -1:, :]
    q_dec = q * jnp.exp(b)
    k_in = k * jnp.exp(-b)
    k_st = k * jnp.exp(b_last - b)
    causal = jnp.tril(jnp.ones((CHUNK, CHUNK), dtype=bool))
    att = jnp.where(causal, jnp.einsum('bhnck,bhnsk->bhncs', q_dec, k_in), 0.0)
    o_intra = jnp.einsum('bhncs,bhnsv->bhncv', att, v)

    def step(state, xs):
        qd, ks, vv, dl = xs
        o = jnp.einsum('bhck,bhkv->bhcv', qd, state)
        state = dl[..., None] * state + jnp.einsum('bhck,bhcv->bhkv', ks, vv)
        return state, o

    xs = (jnp.moveaxis(q_dec, 2, 0), jnp.moveaxis(k_st, 2, 0), jnp.moveaxis(v, 2, 0),
          jnp.moveaxis(jnp.exp(b_last[:, :, :, 0, :]), 2, 0))
    s0 = jnp.zeros((bsz, A_HEADS, HEAD_DIM, HEAD_DIM), f32)
    _, o_inter = lax.scan(step, s0, xs)
    o = o_intra + jnp.moveaxis(o_inter, 0, 2)
    o = o.transpose(0, 2, 3, 1, 4).reshape(bsz, s, A_HEADS, HEAD_DIM)
    o = _rms(o, onorm_g.reshape(A_HEADS, HEAD_DIM)).reshape(bsz, s, WIDTH).astype(h.dtype)
    return (o * jax.nn.silu(g)) @ w_out


def _shared_kv(x, c, kv_mod_w, kv_mod_b, kv_norm_g, kv_w, kv_fb, k_norm_g):
    bsz, s, _ = x.shape
    shift, scale = jnp.split(_ada(c, kv_mod_w, kv_mod_b), 2, axis=-1)
    h = _rms(x, kv_norm_g) * (1.0 + scale[:, None]) + shift[:, None]
    proj = h @ kv_w
    k = proj[..., :WIDTH].reshape(bsz, s, B_HEADS, HEAD_DIM)
    v = proj[..., WIDTH:2 * WIDTH].reshape(bsz, s, B_HEADS, HEAD_DIM)
    fl = proj[..., 2 * WIDTH:] + kv_fb
    k = _rms(k, k_norm_g).transpose(0, 2, 1, 3)
    v = v.transpose(0, 2, 1, 3)
    F = jnp.cumsum(jax.nn.log_sigmoid(fl.astype(jnp.float32)), axis=1).transpose(0, 2, 1)
    return k, v, F


def _fox(h, w_in, q_norm_g, w_out, k, v, F):
    bsz, s, _ = h.shape
    q, g = jnp.split(h @ w_in, 2, axis=-1)
    q = _rms(q.reshape(bsz, s, B_HEADS, HEAD_DIM), q_norm_g).transpose(0, 2, 1, 3)
    scale = HEAD_DIM ** -0.5
    outs = []
    for blk in range(s // Q_BLOCK):
        lo, hi = blk * Q_BLOCK, (blk + 1) * Q_BLOCK
        logits = (jnp.einsum('bhqd,bhkd->bhqk', q[:, :, lo:hi], k[:, :, :hi]).astype(jnp.float32) * scale
                  + (F[:, :, lo:hi, None] - F[:, :, None, :hi]))
        mask = (lo + jnp.arange(Q_BLOCK))[:, None] >= jnp.arange(hi)[None, :]
        p = jax.nn.softmax(jnp.where(mask, logits, -jnp.inf), axis=-1)
        outs.append(jnp.einsum('bhqk,bhkd->bhqd', p.astype(v.dtype), v[:, :, :hi]))
    o = jnp.concatenate(outs, axis=2).transpose(0, 2, 1, 3).reshape(bsz, s, WIDTH)
    return (o * jax.nn.silu(g)) @ w_out


def setup_inputs(seed: int = 0) -> dict:
    key = jax.random.key(seed)
    ks = jax.random.split(key, 20)
    n = jax.random.normal
    D, W = D_MODEL, WIDTH
    return {
        "x": n(ks[0], (BATCH, SEQ, D), jnp.float32),
        "c": n(ks[1], (BATCH, D), jnp.float32),
        "mod_w": n(ks[2], (DEPTH, D, 3 * D), jnp.float32) * (0.3 * D ** -0.5),
        "mod_b": n(ks[3], (DEPTH, 3 * D), jnp.float32) * 0.02,
        "norm_g": 1.0 + 0.02 * n(ks[4], (DEPTH, D), jnp.float32),
        "a_w_in": n(ks[5], (N_A, D, 4 * W), jnp.float32) * D ** -0.5,
        "a_lb_logits": 0.5 * n(ks[6], (N_A + 1, W), jnp.float32),
        "a_onorm_g": 1.0 + 0.02 * n(ks[7], (N_A, W), jnp.float32),
        "a_w_out": n(ks[8], (N_A, W, D), jnp.float32) * W ** -0.5,
        "kv_mod_w": n(ks[9], (D, 2 * D), jnp.float32) * (0.3 * D ** -0.5),
        "kv_mod_b": n(ks[10], (2 * D,), jnp.float32) * 0.02,
        "kv_norm_g": 1.0 + 0.02 * n(ks[11], (D,), jnp.float32),
        "kv_w": n(ks[12], (D, 2 * W + B_HEADS), jnp.float32) * D ** -0.5,
        "kv_fb": 2.0 + 0.5 * n(ks[13], (B_HEADS,), jnp.float32),
        "k_norm_g": 1.0 + 0.02 * n(ks[14], (HEAD_DIM,), jnp.float32),
        "b_w_in": n(ks[15], (N_B, D, 2 * W), jnp.float32) * D ** -0.5,
        "b_q_norm_g": 1.0 + 0.02 * n(ks[16], (N_B, HEAD_DIM), jnp.float32),
        "b_w_out": n(ks[17], (N_B, W, D), jnp.float32) * W ** -0.5,
    }


def reference(x, c, mod_w, mod_b, norm_g, a_w_in, a_lb_logits, a_onorm_g, a_w_out,
              kv_mod_w, kv_mod_b, kv_norm_g, kv_w, kv_fb, k_norm_g,
              b_w_in, b_q_norm_g, b_w_out):
    lb_all = jnp.cumsum(jax.nn.softmax(a_lb_logits.astype(jnp.float32), axis=0), axis=0)
    kv = None
    for l in range(DEPTH):
        shift, scale, gate = jnp.split(_ada(c, mod_w[l], mod_b[l]), 3, axis=-1)
        h = _rms(x, norm_g[l]) * (1.0 + scale[:, None]) + shift[:, None]
        if l < N_A:
            y = _hgrn2(h, a_w_in[l], lb_all[l], a_onorm_g[l], a_w_out[l])
        else:
            if l == N_A:
                kv = _shared_kv(x, c, kv_mod_w, kv_mod_b, kv_norm_g, kv_w, kv_fb, k_norm_g)
            j = l - N_A
            y = _fox(h, b_w_in[j], b_q_norm_g[j], b_w_out[j], *kv)
        x = x + gate[:, None] * y
    return x
```

```python
import bisect
import math
from contextlib import ExitStack

import numpy as np
import concourse.bass as bass
import concourse.mybir as mybir
from concourse.bass_utils import run_bass_kernel_spmd

F32 = mybir.dt.float32
BF = mybir.dt.bfloat16
AF = mybir.ActivationFunctionType
ALU = mybir.AluOpType

D = 1024
W = 2048
H = 16
EPS = 1e-6
NEG = -30000.0

EPOCH = 2000
DMA_EPOCH = 120


def _region(ap):
    pat = ap.ap
    off = ap.offset
    name = ap.name
    if str(ap.space) == "DRAM":
        hi = off
        for st, cnt in pat:
            hi += st * (cnt - 1)
        return (name, 0, 1, off, hi + 1)
    if str(ap.space) == "PSUM":
        return (name, 0, 128, 0, 1 << 40, True)
    pstep, pcnt = pat[0]
    if pstep == 0:
        return (name, 0, 128, 0, 1 << 40)
    p0 = off // pstep
    f0 = off - p0 * pstep
    f1 = f0
    for st, cnt in pat[1:]:
        f1 += st * (cnt - 1)
    es = mybir.dt.size(ap.dtype)
    return (name, p0, p0 + pcnt, f0 * es, (f1 + 1) * es)


def _overlap(a, b):
    return a[1] < b[2] and b[1] < a[2] and a[3] < b[4] and b[3] < a[4]


def _covers(big, small):
    return big[1] <= small[1] and big[2] >= small[2] and big[3] <= small[3] and big[4] >= small[4]


class Op:
    __slots__ = ("eng", "fn", "waits", "need_inc", "dma_key", "stream", "sidx")


class Prog:
    ENGS = ("pe", "act", "dve", "pool", "sp")

    def __init__(self, nc):
        self.nc = nc
        self.q = {e: [] for e in self.ENGS}
        self.stream_ops = {}
        self.known = {e: {} for e in self.ENGS}
        self.ckpt = {}
        self.tw = {}
        self.tr = {}
        self.kver = {e: 0 for e in self.ENGS}
        self.ckver = {}

    def _add(self, queue, fn, reads, writes, dma_key=None):
        op = Op()
        op.eng = queue
        op.fn = fn
        op.dma_key = dma_key
        op.need_inc = dma_key is not None
        stream = dma_key if dma_key is not None else queue
        op.stream = stream
        lst = self.stream_ops.setdefault(stream, [])
        op.sidx = len(lst)
        lst.append(op)
        deps = {}

        def add_dep(o, kind):
            if o.stream == stream and dma_key is None:
                if queue == "pe":
                    return
            if deps.get(o.stream, -1) < o.sidx:
                deps[o.stream] = o.sidx

        rregs = [_region(a) for a in reads]
        wregs = [_region(a) for a in writes]
        if dma_key is not None and op.sidx > 0:
            deps[stream] = op.sidx - 1
        for r in rregs:
            for (wr, wo) in self.tw.get(r[0], ()):
                if _overlap(wr, r):
                    add_dep(wo, "raw")
            if len(r) > 5:
                for (rr, ro) in self.tr.get(r[0], ()):
                    if ro.stream != stream:
                        add_dep(ro, "rar")
        for w in wregs:
            for (wr, wo) in self.tw.get(w[0], ()):
                if _overlap(wr, w):
                    add_dep(wo, "waw")
            for (rr, ro) in self.tr.get(w[0], ()):
                if _overlap(rr, w):
                    add_dep(ro, "war")
        for w in wregs:
            tw = self.tw.setdefault(w[0], [])
            tw[:] = [x for x in tw if not _covers(w, x[0])]
            tw.append((w, op))
            tr = self.tr.get(w[0])
            if tr:
                tr[:] = [x for x in tr if not _covers(w, x[0])]
        for r in rregs:
            tr = self.tr.setdefault(r[0], [])
            tr[:] = [x for x in tr if not (x[1].stream == stream and _covers(r, x[0]))]
            tr.append((r, op))
        kn = self.known[queue]
        waits = [(s, i) for s, i in deps.items() if kn.get(s, -1) < i]
        for s, i in waits:
            self.stream_ops[s][i].need_inc = True
            if kn.get(s, -1) < i:
                kn[s] = i
            ck = self.ckpt.get(s)
            if ck:
                j = bisect.bisect_right(ck[0], i) - 1
                if j >= 0:
                    for s2, i2 in ck[1][j].items():
                        if kn.get(s2, -1) < i2:
                            kn[s2] = i2
        op.waits = waits
        if waits:
            self.kver[queue] += 1
        if self.ckver.get(stream) != self.kver[queue]:
            self.ckver[stream] = self.kver[queue]
            ck = self.ckpt.setdefault(stream, ([], []))
            snap = dict(kn)
            if dma_key is None:
                snap[stream] = op.sidx - 1
            ck[0].append(op.sidx)
            ck[1].append(snap)
        self.q[queue].append(op)
        return op

    def dma(self, key, out, in_, queue="sp"):
        return self._add(queue, lambda e: e.dma_start(out=out, in_=in_), [in_], [out], dma_key=key)

    def emit(self, final_waits=()):
        nc = self.nc
        sem_of = {}
        val_of = {}
        stack = ExitStack()
        nsem = 0
        for stream, ops in self.stream_ops.items():
            is_dma = ops[0].dma_key is not None
            ep_len = DMA_EPOCH if is_dma else EPOCH
            step = 16 if is_dma else 1
            cnt = 0
            for o in ops:
                if not o.need_inc:
                    continue
                ep = cnt // ep_len
                if (stream, ep) not in sem_of:
                    sem_of[(stream, ep)] = stack.enter_context(nc.semaphore("s%d" % nsem))
                    nsem += 1
                val_of[o] = (sem_of[(stream, ep)], (cnt % ep_len + 1) * step, step)
                cnt += 1
        self.nsem = nsem
        engines = {"pe": "tensor", "act": "scalar", "dve": "vector", "pool": "gpsimd", "sp": "sync"}
        fw = [val_of[o] for o in final_waits]
        with stack:
            with nc.Block() as block:
                for qn in self.ENGS:
                    ops = self.q[qn]
                    if not ops:
                        continue

                    def body(eng, ops=ops, qn=qn):
                        for o in ops:
                            for (s, i) in o.waits:
                                sem, val, _ = val_of[self.stream_ops[s][i]]
                                eng.wait_ge(sem, val)
                            inst = o.fn(eng)
                            if o.need_inc:
                                sem, val, step = val_of[o]
                                inst.then_inc(sem, step)
                        if qn == "sp":
                            for (sem, val, _) in fw:
                                eng.wait_ge(sem, val)

                    getattr(block, engines[qn])(body)


class Arena:
    def __init__(self, ap_bf, nbytes):
        self.ap = ap_bf
        self.cap = nbytes
        self.off = 0

    def get(self, shape, dt, parts=128):
        n = 1
        for s in shape:
            n *= s
        nb = n * mybir.dt.size(dt)
        a = self.ap[0:parts, self.off // 2:(self.off + nb) // 2]
        if dt != BF:
            a = a.bitcast(dt)
        if len(shape) == 2:
            a = a.rearrange("p (a b) -> p a b", b=shape[1])
        elif len(shape) == 3:
            a = a.rearrange("p (a b c) -> p a b c", b=shape[1], c=shape[2])
        self.off += (nb + 63) // 64 * 64
        assert self.off <= self.cap, (self.off, self.cap)
        return a


def build(S=4096, dbg=False, stop=None):
    NT = S // 128
    NST = S // 512
    nc = bass.Bass("TRN2", target_bir_lowering=False)
    es = ExitStack()
    es.enter_context(nc.allow_low_precision("bf16 matmul operands, fp32 accumulation"))

    def din(name, shape):
        return nc.dram_tensor(name, shape, F32, kind="ExternalInput").ap()

    x = din("x", [S, D])
    c_col = din("c_col", [128, 8])
    mod_w0 = din("mod_w0", [D, 3 * D])
    mod_w1 = din("mod_w1", [D, 3 * D])
    modb_col = din("modb_col", [128, 48])
    ng_col = din("ng_col", [128, 16])
    a_w_in = din("a_w_in", [D, 4 * W])
    alb_col = din("alb_col", [128, 32])
    aon_col = din("aon_col", [128, 16])
    a_w_out = din("a_w_out", [W, D])
    kv_mod_w = din("kv_mod_w", [D, 2 * D])
    kvmb_col = din("kvmb_col", [128, 16])
    kvng_col = din("kvng_col", [128, 8])
    kv_w = din("kv_w", [D, 2 * W + H])
    kv_fb = din("kv_fb", [H, 1])
    kng = din("kng", [128, 1])
    b_w_in = din("b_w_in", [D, 2 * W])
    qng = din("qng", [128, 1])
    b_w_out = din("b_w_out", [W, D])
    outp = nc.dram_tensor("out", [S, D], F32, kind="ExternalOutput").ap()
    dk = "ExternalOutput" if dbg else "Internal"
    uT = nc.dram_tensor("uT", [W, S], BF, kind="Internal").ap()
    x1 = nc.dram_tensor("x1", [S, D], F32, kind=dk).ap()
    KTs = nc.dram_tensor("KTs", [W, S], BF, kind="Internal").ap()
    Vs = nc.dram_tensor("Vs", [S, W], BF, kind="Internal").ap()

    P = Prog(nc)
    ARENA = 124 * 1024
    arena_t = es.enter_context(nc.sbuf_tensor("arena", [128, ARENA // 2], BF))
    hT = es.enter_context(nc.sbuf_tensor("hT", [128, 8, S], BF))
    cst = es.enter_context(nc.sbuf_tensor("cst", [128, 2560], F32))
    FT = es.enter_context(nc.sbuf_tensor("FT", [128, NT, 16], F32))
    Fd = nc.dram_tensor("Fd", [H, S], F32, kind="Internal").ap()
    PS = [es.enter_context(nc.psum_tensor("ps%d" % i, [128, 512], F32)) for i in range(8)]
    A = Arena(arena_t[:], ARENA)
    C = Arena(cst[:].bitcast(BF), 10240)

    def aps(*xs):
        return [a for a in xs if a is not None and not isinstance(a, (int, float))]

    def MM(out, lhsT, rhs, start=True, stop=True, skip=False):
        if skip:
            P._add("pe", lambda e: e.matmul(out, lhsT, rhs, start=start, stop=stop, skip_group_check=True),
                   [lhsT, rhs], [out])
        else:
            P._add("pe", lambda e: e.matmul(out, lhsT, rhs, start=start, stop=stop), [lhsT, rhs], [out])

    def TR(out, in_, ident):
        P._add("pe", lambda e: e.transpose(out, in_, ident), [in_, ident], [out])

    def ACT(out, in_, func, bias=0.0, scale=1.0, accum_out=None):
        P._add("act", lambda e: e.activation(out=out, in_=in_, func=func, bias=bias, scale=scale,
                                             accum_out=accum_out),
               aps(in_, bias, scale), aps(out, accum_out))

    def TT(out, in0, in1, op, eng="dve"):
        P._add(eng, lambda e: e.tensor_tensor(out=out, in0=in0, in1=in1, op=op), [in0, in1], [out])

    def TS(out, in0, s1, op0, s2=None, op1=None, eng="dve"):
        if op1 is None:
            P._add(eng, lambda e: e.tensor_scalar(out=out, in0=in0, scalar1=s1, scalar2=None, op0=op0),
                   aps(in0, s1), [out])
        else:
            P._add(eng, lambda e: e.tensor_scalar(out=out, in0=in0, scalar1=s1, scalar2=s2, op0=op0, op1=op1),
                   aps(in0, s1, s2), [out])

    def STT(out, in0, scalar, in1, op0, op1, eng="dve"):
        P._add(eng, lambda e: e.scalar_tensor_tensor(out=out, in0=in0, scalar=scalar, in1=in1, op0=op0, op1=op1),
               aps(in0, scalar, in1), [out])

    def COPY(out, in_, eng="dve"):
        if eng == "act":
            P._add("act", lambda e: e.copy(out=out, in_=in_), [in_], [out])
        else:
            P._add(eng, lambda e: e.tensor_copy(out=out, in_=in_), [in_], [out])

    def RECIP(out, in_):
        P._add("dve", lambda e: e.reciprocal(out=out, in_=in_), [in_], [out])

    def MEMSET(ap, v, eng="pool"):
        P._add(eng, lambda e: e.memset(ap, v), [], [ap])

    def ASEL(out, in_, pattern, op, fill, base, cm):
        P._add("pool", lambda e: e.affine_select(out=out, in_=in_, pattern=pattern, compare_op=op, fill=fill,
                                                 base=base, channel_multiplier=cm), [in_], [out])

    dcount = {}

    def DMA(cls, out, in_, nrot=1):
        k = dcount.get(cls, 0)
        dcount[cls] = k + 1
        return P.dma("%s%d" % (cls, k % nrot), out, in_)

    ident_f = C.get([128], F32)
    ident_b = C.get([128], BF)
    ones_b = C.get([128], BF)
    ones_f = C.get([128], F32)
    cmask = C.get([512], F32)
    amask = C.get([128], F32)
    cbias = C.get([128], F32)
    c_sb = C.get([8], F32)
    sc_sb = C.get([8], F32)
    modb = C.get([48], F32)
    ngc = C.get([16], F32)
    albc = C.get([32], F32)
    aonc = C.get([16], F32)
    kvmbc = C.get([16], F32)
    kvngc = C.get([8], F32)
    kngc = C.get([1], F32)
    qngc = C.get([1], F32)
    kvfbc = C.get([1], F32, parts=16)
    mod0 = C.get([24], F32)
    mod1 = C.get([24], F32)
    kvmod = C.get([16], F32)
    A0 = C.get([8], F32)
    A1 = C.get([8], F32)
    Akv = C.get([8], F32)
    lbc = C.get([16], F32)
    omlc = C.get([16], F32)
    nomlc = C.get([16], F32)
    qgs = C.get([1], F32)
    tmpc = C.get([32], F32)
    epsc = C.get([1], F32)
    ones512 = C.get([512], F32)

    MEMSET(epsc, EPS)
    MEMSET(ones512, 1.0)
    MEMSET(ident_f, 0.0)
    ASEL(ident_f, ident_f, [[-1, 128]], ALU.not_equal, 1.0, 0, 1)
    COPY(ident_b, ident_f)
    MEMSET(ones_b, 1.0)
    MEMSET(ones_f, 1.0)
    MEMSET(cmask, 1.0)
    MEMSET(cmask.rearrange("p (c t) -> p c t", t=64)[:, :, 0:1], 0.0)
    MEMSET(amask, 1.0)
    ASEL(amask, amask, [[1, 128]], ALU.is_ge, 0.0, 0, -1)
    MEMSET(amask[0:64, 64:128], 0.0)
    MEMSET(cbias, 0.0)
    ASEL(cbias, cbias, [[1, 128]], ALU.is_ge, NEG, 0, -1)
    for i, (dst, src) in enumerate([(c_sb, c_col), (modb, modb_col), (ngc, ng_col), (albc, alb_col),
                                    (aonc, aon_col), (kvmbc, kvmb_col), (kvngc, kvng_col), (kngc, kng),
                                    (qngc, qng), (kvfbc, kv_fb)]):
        P.dma("cst%d" % i, dst, src)

    ACT(sc_sb, c_sb, AF.Silu)

    def modcalc(wsrc, ncol, bcol, dst):
        A.off = 0
        nch = ncol // 128
        wt = A.get([8, ncol], F32)
        wv = wsrc.rearrange("(kc p) n -> p kc n", p=128)
        for kc in range(8):
            P.dma("mw%d" % (kc % 4), wt[:, kc, :], wv[:, kc, :])
        ps = PS[0]
        for m in range(nch):
            for kc in range(8):
                MM(ps[:, m:m + 1], wt[:, kc, m * 128:(m + 1) * 128], sc_sb[:, kc:kc + 1],
                   start=(kc == 0), stop=(kc == 7))
        TT(dst, ps[:, 0:nch], bcol, ALU.add)

    def modcalc_bg(wsrc, ncol, bcol, dst, off):
        wv = wsrc.rearrange("(kc p) n -> p kc n", p=128)
        save = A.off
        A.off = off
        regs = [A.get([8, 512], F32) for _ in range(2)]
        A.off = save
        ps = PS[0]
        for ci, c0 in enumerate(range(0, ncol, 512)):
            wt = regs[ci % 2]
            for half in range(2):
                P.dma("mwb%d" % ((2 * ci + half) % 4), wt[:, half * 4:(half + 1) * 4, :],
                      wv[:, half * 4:(half + 1) * 4, c0:c0 + 512], queue="pool")
            yield
            for m in range(4):
                mg = c0 // 128 + m
                for kc in range(8):
                    MM(ps[:, mg:mg + 1], wt[:, kc, m * 128:(m + 1) * 128], sc_sb[:, kc:kc + 1],
                       start=(kc == 0), stop=(kc == 7))
            yield
        TT(dst, ps[:, 0:ncol // 128], bcol, ALU.add)
        yield

    modcalc(mod_w0, 3 * D, modb[:, 0:24], mod0)
    STT(A0, mod0[:, 8:16], 1.0, ngc[:, 0:8], ALU.add, ALU.mult)
    TT(tmpc[:, 0:16], albc[:, 0:16], albc[:, 16:32], ALU.subtract)
    ACT(lbc, tmpc[:, 0:16], AF.Sigmoid)
    ACT(omlc, tmpc[:, 0:16], AF.Sigmoid, scale=-1.0)
    TS(nomlc, omlc, -1.0, ALU.mult)
    TS(qgs, qngc, 1.0 / math.sqrt(128.0), ALU.mult)

    def prepass_p1(xt, ss, xn):
        ACT(prepass_p1.junk, xt, AF.Square, accum_out=ss[:, 0:1])
        ACT(ss[:, 1:2], ss[:, 0:1], AF.Ln, bias=epsc, scale=1.0 / D)
        ACT(ss[:, 2:3], ss[:, 1:2], AF.Exp, scale=-0.5)
        TS(xn, xt, ss[:, 2:3], ALU.mult)

    def prepass_p2(xn, tt, Acol, shcol, pbank):
        for kc in range(8):
            pb = PS[pbank + kc // 4][:, 0:256].bitcast(BF)[:, (kc % 4) * 128:(kc % 4 + 1) * 128]
            TR(pb, xn[:, kc * 128:(kc + 1) * 128], ident_b)
        for kc in range(8):
            pb = PS[pbank + kc // 4][:, 0:256].bitcast(BF)[:, (kc % 4) * 128:(kc % 4 + 1) * 128]
            if kc < 4:
                ACT(hT[:, kc, tt * 128:(tt + 1) * 128], pb, AF.Identity, bias=shcol[:, kc:kc + 1],
                    scale=Acol[:, kc:kc + 1])
            else:
                TS(hT[:, kc, tt * 128:(tt + 1) * 128], pb, Acol[:, kc:kc + 1], ALU.mult,
                   shcol[:, kc:kc + 1], ALU.add)

    def prepass_from_dram(src, Acol, shcol, bg=None):
        A.off = 0
        xts = [A.get([D], F32) for _ in range(3)]
        prepass_p1.junk = A.get([D], F32)
        xns = [A.get([D], BF) for _ in range(2)]
        sss = [A.get([4], F32) for _ in range(2)]
        for tt in range(NT):
            xt = xts[tt % 3]
            DMA("xin", xt, src[tt * 128:(tt + 1) * 128, :], nrot=2)
            prepass_p1(xt, sss[tt % 2], xns[tt % 2])
            if tt > 0:
                prepass_p2(xns[(tt - 1) % 2], tt - 1, Acol, shcol, 4 + 2 * ((tt - 1) % 2))
            if bg is not None and next(bg, "done") == "done":
                bg = None
        prepass_p2(xns[(NT - 1) % 2], NT - 1, Acol, shcol, 4 + 2 * ((NT - 1) % 2))
        if bg is not None:
            for _ in bg:
                pass

    def load_w(src2d, c0, ncols, dst, stage):
        sv = src2d.rearrange("(kc p) n -> p kc n", p=128)
        for half in range(2):
            k = load_w.n
            load_w.n += 1
            st = stage[k % 2]
            P.dma("wst%d" % (k % 2), st[:, :, 0:ncols], sv[:, half * 4:(half + 1) * 4, c0:c0 + ncols])
            COPY(dst[:, half * 4:(half + 1) * 4, :], st[:, :, 0:ncols], eng="pool" if half == 0 else "act")
    load_w.n = 0

    def out_phase(w_out, xsrc, gcol, dst, next_pre=None):
        A.off = 0
        wo = A.get([16, D], BF)
        stage = [A.get([4, 512], F32) for _ in range(2)]
        wov = w_out.rearrange("(kc p) n -> p kc n", p=128)
        k = 0
        for q4 in range(4):
            for nh in range(2):
                st = stage[k % 2]
                P.dma("wst%d" % (k % 2), st, wov[:, q4 * 4:(q4 + 1) * 4, nh * 512:(nh + 1) * 512])
                COPY(wo[:, q4 * 4:(q4 + 1) * 4, nh * 512:(nh + 1) * 512], st, eng="pool")
                k += 1
        grow = A.get([D], F32)
        dg = A.get([128], F32)
        for kc in range(8):
            TS(dg, ident_f, gcol[:, kc:kc + 1], ALU.mult)
            pb = PS[0][:, 0:128]
            MM(pb, ones_f, dg)
            COPY(grow[:, kc * 128:(kc + 1) * 128], pb)
        uts = [A.get([16, 512], BF) for _ in range(2)]
        xts = [A.get([D], F32) for _ in range(2)]
        xos = [A.get([D], F32) for _ in range(3)]
        prepass_p1.junk = A.get([D], F32)
        xns = [A.get([D], BF) for _ in range(2)]
        sss = [A.get([4], F32) for _ in range(2)]
        uv = uT.rearrange("(kc p) s -> p kc s", p=128)
        outs = []
        for st_ in range(NST):
            ut = uts[st_ % 2]
            DMA("uin", ut, uv[:, :, st_ * 512:(st_ + 1) * 512], nrot=2)
            for t4 in range(4):
                tt = st_ * 4 + t4
                xt = xts[tt % 2]
                xo = xos[tt % 3]
                DMA("xin", xt, xsrc[tt * 128:(tt + 1) * 128, :], nrot=2)
                pb0 = 2 * (tt % 2)
                for nh in range(2):
                    for kc in range(16):
                        MM(PS[pb0 + nh][:, :], ut[:, kc, t4 * 128:(t4 + 1) * 128],
                           wo[:, kc, nh * 512:(nh + 1) * 512], start=(kc == 0), stop=(kc == 15))
                for nh in range(2):
                    sl = slice(nh * 512, (nh + 1) * 512)
                    TT(xo[:, sl], PS[pb0 + nh][:, :], grow[:, sl], ALU.mult)
                    TT(xo[:, sl], xo[:, sl], xt[:, sl], ALU.add, eng="pool")
                outs.append(DMA("xst", dst[tt * 128:(tt + 1) * 128, :], xo, nrot=4))
                if next_pre is not None:
                    prepass_p1(xo, sss[tt % 2], xns[tt % 2])
                    if tt > 0:
                        prepass_p2(xns[(tt - 1) % 2], tt - 1, next_pre[0], next_pre[1], 4 + 2 * ((tt - 1) % 2))
        if next_pre is not None:
            prepass_p2(xns[(NT - 1) % 2], NT - 1, next_pre[0], next_pre[1], 4 + 2 * ((NT - 1) % 2))
        return outs

    def _stop():
        P.emit()
        es.close()
        return nc, P

    if stop == "mod":
        return _stop()
    def _bg_mods():
        yield from modcalc_bg(kv_mod_w, 2 * D, kvmbc, kvmod, 24 * 1024)
        STT(Akv, kvmod[:, 8:16], 1.0, kvngc, ALU.add, ALU.mult)
        yield from modcalc_bg(mod_w1, 3 * D, modb[:, 24:48], mod1, 24 * 1024)
        STT(A1, mod1[:, 8:16], 1.0, ngc[:, 8:16], ALU.add, ALU.mult)

    prepass_from_dram(x, A0, mod0[:, 0:8], bg=_bg_mods())
    if stop == "pre":
        return _stop()

    A.off = 0
    stage = [A.get([4, 512], F32) for _ in range(2)]
    wq = A.get([8, 512], BF)
    wf = A.get([8, 512], BF)
    wi = A.get([8, 512], BF)
    wg = A.get([8, 512], BF)
    vg = A.get([NT, 512], BF)
    NB = 2
    T1 = [A.get([512], F32) for _ in range(NB)]
    T2 = [A.get([512], F32) for _ in range(NB)]
    T3 = [A.get([512], F32) for _ in range(NB)]
    T4 = [A.get([512], F32) for _ in range(NB)]
    T5 = [A.get([512], F32) for _ in range(NB)]
    qdec = [A.get([512], BF) for _ in range(NB)]
    kinb = [A.get([512], BF) for _ in range(NB)]
    kstb = [A.get([512], BF) for _ in range(NB)]
    sgb = [A.get([512], BF) for _ in range(NB)]
    sqb = [A.get([512], BF) for _ in range(NB)]
    ub = [A.get([512], BF) for _ in range(NB)]
    kstT = [A.get([4, 128], BF) for _ in range(NB)]
    attb = [A.get([128], BF) for _ in range(4)]
    S32 = [A.get([128], F32) for _ in range(2)]
    Sbf = [A.get([128], BF) for _ in range(2)]

    def stageA(h, j, st_, b_):
        hs = slice(j * 128, (j + 1) * 128)
        ts_ = slice(st_ * 512, (st_ + 1) * 512)
        for (pb, wt) in ((PS[1], wf), (PS[2], wg), (PS[0], wq)):
            for kc in range(8):
                MM(pb[:, :], wt[:, kc, hs], hT[:, kc, ts_], start=(kc == 0), stop=(kc == 7))
            yield
        t1, t2, t3, t4_, t5 = T1[b_], T2[b_], T3[b_], T4[b_], T5[b_]
        ACT(t1, PS[1][:, :], AF.Sigmoid)
        ACT(sqb[b_], PS[2][:, :], AF.Sigmoid)
        yield
        TS(t2, t1, nomlc[:, h:h + 1], ALU.mult, omlc[:, h:h + 1], ALU.add, eng="pool")
        TT(sgb[b_], PS[2][:, :], sqb[b_], ALU.mult)
        COPY(t5, PS[0][:, :], eng="act")
        yield
        ACT(t1, t1, AF.Ln, bias=lbc[:, h:h + 1], scale=omlc[:, h:h + 1])
        yield
        P._add("dve", lambda e, o=t3, d1=t1: e.tensor_tensor_scan(out=o, data0=cmask, data1=d1, initial=0.0,
                                                                  op0=ALU.mult, op1=ALU.add),
               [cmask, t1], [t3])
        yield
        ACT(t4_, t3, AF.Exp)
        ACT(t1, t3, AF.Exp, scale=-1.0)
        yield
        TT(qdec[b_], t5, t4_, ALU.mult, eng="pool")
        TT(t2, t2, t1, ALU.mult, eng="pool")
        yield
        COPY(kinb[b_], t2, eng="pool")
        eb3 = t4_.rearrange("p (c t) -> p c t", t=64)
        TT(kstb[b_].rearrange("p (c t) -> p c t", t=64), t2.rearrange("p (c t) -> p c t", t=64),
           eb3[:, :, 63:64].to_broadcast([128, 8, 64]), ALU.mult, eng="pool")
        yield
        ptr = PS[6][:, 0:256].bitcast(BF)
        for q4 in range(4):
            TR(ptr[:, q4 * 128:(q4 + 1) * 128], kstb[b_][:, q4 * 128:(q4 + 1) * 128], ident_b)
        COPY(kstT[b_].rearrange("p a b -> p (a b)"), ptr, eng="act")
        yield

    sidx_box = [0]

    def stageB(h, j, st_, b_):
        hs = slice(j * 128, (j + 1) * 128)
        ts_ = slice(st_ * 512, (st_ + 1) * 512)
        t1, t2, t4_ = T1[b_], T2[b_], T4[b_]
        po = PS[3]
        if st_ == 0:
            sidx_box[0] = 0
        for q4 in range(4):
            tt = st_ * 4 + q4
            cs = slice(q4 * 128, (q4 + 1) * 128)
            pa = PS[4][:, 0:128]
            MM(pa, kinb[b_][:, cs], qdec[b_][:, cs])
            ab = attb[q4]
            TT(ab, pa, amask, ALU.mult)
            for jj in range(2):
                sidx = sidx_box[0]
                n = st_ * 8 + q4 * 2 + jj
                rs = slice(jj * 64, (jj + 1) * 64)
                oc = slice(q4 * 128 + jj * 64, q4 * 128 + (jj + 1) * 64)
                first = (n == 0)
                pu = PS[5 if n % 2 == 0 else 7][:, 0:128]
                MM(pu, kstT[b_][rs, q4, :], vg[rs, tt, hs])
                MM(po[:, oc], vg[rs, tt, hs], ab[rs, rs], start=True, stop=first)
                if not first:
                    MM(po[:, oc], Sbf[sidx % 2], qdec[b_][:, oc], start=False, stop=True)
                dcol = t4_[:, q4 * 128 + jj * 64 + 63:q4 * 128 + jj * 64 + 64]
                if first:
                    COPY(Sbf[(sidx + 1) % 2], pu)
                    COPY(S32[(sidx + 1) % 2], pu)
                else:
                    STT(Sbf[(sidx + 1) % 2], S32[sidx % 2], dcol, pu, ALU.mult, ALU.add)
                    STT(S32[(sidx + 1) % 2], S32[sidx % 2], dcol, pu, ALU.mult, ALU.add)
                sidx_box[0] = sidx + 1
                yield
        ACT(sqb[b_], po[:, :], AF.Square)
        MM(PS[6][:, :], ones_b, sqb[b_])
        yield
        ACT(t1, PS[6][:, :], AF.Ln, bias=epsc, scale=1.0 / 128.0)
        ACT(t1, t1, AF.Exp, scale=-0.5)
        yield
        TT(t2, po[:, :], t1, ALU.mult)
        STT(ub[b_], t2, aonc[:, h:h + 1], sgb[b_], ALU.mult, ALU.mult)
        DMA("ust", uT[h * 128:(h + 1) * 128, ts_], ub[b_], nrot=4)
        yield

    def interleave(ga, gb, pre=3):
        done_a = ga is None
        done_b = gb is None
        if not done_a:
            for _ in range(pre):
                try:
                    next(ga)
                except StopIteration:
                    done_a = True
                    break
        while not (done_a and done_b):
            if not done_b:
                try:
                    next(gb)
                except StopIteration:
                    done_b = True
            if not done_a:
                try:
                    next(ga)
                except StopIteration:
                    done_a = True

    hb = 0
    for g in range(4):
        load_w(a_w_in, 0 * W + g * 512, 512, wq, stage)
        load_w(a_w_in, 1 * W + g * 512, 512, wf, stage)
        load_w(a_w_in, 2 * W + g * 512, 512, wi, stage)
        load_w(a_w_in, 3 * W + g * 512, 512, wg, stage)
        for tt in range(NT):
            pv = PS[6 + tt % 2]
            for kc in range(8):
                MM(pv[:, :], hT[:, kc, tt * 128:(tt + 1) * 128], wi[:, kc, :], start=(kc == 0), stop=(kc == 7))
            if tt % 2 == 0:
                COPY(vg[:, tt, :], pv[:, :], eng="act")
            else:
                COPY(vg[:, tt, :], pv[:, :])
        tiles = [(g * 4 + j, j, st_) for j in range(4) for st_ in range(NST)]
        prev = None
        for (h, j, st_) in tiles:
            b_ = hb % NB
            hb += 1
            ga = stageA(h, j, st_, b_)
            gb = stageB(*prev) if prev is not None else None
            interleave(ga, gb)
            prev = (h, j, st_, b_)
        interleave(None, stageB(*prev))

    if stop == "l0b":
        return _stop()
    out_phase(a_w_out, x, mod0[:, 16:24], x1, next_pre=(Akv, kvmod[:, 0:8]))

    if stop == "l0c":
        return _stop()
    A.off = 0
    stage = [A.get([4, 512], F32) for _ in range(2)]
    wk = A.get([8, 512], BF)
    wv = A.get([8, 512], BF)
    wfl = A.get([8, 16], BF)
    vst = [A.get([4, 512], BF) for _ in range(2)]
    T1 = [A.get([512], F32) for _ in range(2)]
    sqb = [A.get([512], BF) for _ in range(2)]
    ktb = [A.get([512], BF) for _ in range(2)]
    Fall = A.get([S], F32, parts=16)
    lsg = A.get([512], F32, parts=16)

    sv = kv_w.rearrange("(kc p) n -> p kc n", p=128)
    P.dma("wst0", stage[0][:, :, 0:16], sv[:, 0:4, 2 * W:2 * W + 16])
    COPY(wfl[:, 0:4, :], stage[0][:, :, 0:16], eng="pool")
    P.dma("wst1", stage[1][:, :, 0:16], sv[:, 4:8, 2 * W:2 * W + 16])
    COPY(wfl[:, 4:8, :], stage[1][:, :, 0:16], eng="pool")
    for st_ in range(NST):
        ts_ = slice(st_ * 512, (st_ + 1) * 512)
        pf = PS[4][0:16, :]
        for kc in range(8):
            MM(pf, wfl[:, kc, :], hT[:, kc, ts_], start=(kc == 0), stop=(kc == 7))
        ACT(lsg, pf, AF.Sigmoid, bias=kvfbc)
        ACT(lsg, lsg, AF.Ln)
        init = 0.0 if st_ == 0 else Fall[:, st_ * 512 - 1:st_ * 512]
        P._add("dve", lambda e, o=Fall[:, ts_], ini=init: e.tensor_tensor_scan(
            out=o, data0=ones512[0:16, :], data1=lsg, initial=ini, op0=ALU.mult, op1=ALU.add),
            aps(lsg, init, ones512[0:16, :]), [Fall[:, ts_]])
    P.dma("fst0", Fd, Fall)
    for kb in range(NT):
        pt = PS[5][:, 0:16]
        TR(pt, Fall[:, kb * 128:(kb + 1) * 128], ident_f[0:16, 0:16])
        COPY(FT[:, kb, :], pt)

    for g in range(4):
        load_w(kv_w, g * 512, 512, wk, stage)
        load_w(kv_w, W + g * 512, 512, wv, stage)
        for tt in range(NT):
            pv = PS[2 + tt % 2]
            for kc in range(8):
                MM(pv[:, :], hT[:, kc, tt * 128:(tt + 1) * 128], wv[:, kc, :], start=(kc == 0), stop=(kc == 7))
            vs_ = vst[(tt // 4) % 2]
            if tt % 2 == 0:
                COPY(vs_[:, tt % 4, :], pv[:, :], eng="act")
            else:
                COPY(vs_[:, tt % 4, :], pv[:, :])
            if tt % 4 == 3:
                t0 = tt - 3
                DMA("vst", Vs[t0 * 128:(t0 + 4) * 128, g * 512:(g + 1) * 512].rearrange("(a p) n -> p a n", p=128),
                    vs_, nrot=2)
        for j in range(4):
            h = g * 4 + j
            hs = slice(j * 128, (j + 1) * 128)
            for st_ in range(NST):
                b_ = st_ % 2
                ts_ = slice(st_ * 512, (st_ + 1) * 512)
                pk = PS[0 if b_ == 0 else 5]
                pss = PS[1 if b_ == 0 else 6]
                for kc in range(8):
                    MM(pk[:, :], wk[:, kc, hs], hT[:, kc, ts_], start=(kc == 0), stop=(kc == 7))
                ACT(sqb[b_], pk[:, :], AF.Square)
                MM(pss[:, :], ones_b, sqb[b_])
                ACT(T1[b_], pss[:, :], AF.Ln, bias=epsc, scale=1.0 / 128.0)
                ACT(T1[b_], T1[b_], AF.Exp, scale=-0.5)
                STT(ktb[b_], pk[:, :], kngc[:, 0:1], T1[b_], ALU.mult, ALU.mult)
                DMA("kst", KTs[h * 128:(h + 1) * 128, ts_], ktb[b_], nrot=2)

    if stop == "kv":
        return _stop()
    prepass_from_dram(x1, A1, mod1[:, 0:8])

    A.off = 0
    stage = [A.get([4, 512], F32) for _ in range(2)]
    wq = A.get([8, 512], BF)
    wg = A.get([8, 512], BF)
    KT = A.get([S], BF)
    Vaug = A.get([NT, 130], BF)
    QT = A.get([S], BF)
    sgT = A.get([S], BF)
    Fb = A.get([S], F32)
    nFc = A.get([NT], F32)
    TMP = [A.get([512], F32) for _ in range(6)]
    PT = [A.get([512], BF) for _ in range(6)]
    T1 = [A.get([512], F32) for _ in range(2)]
    sqb = [A.get([512], BF) for _ in range(2)]
    E1 = [A.get([512], F32) for _ in range(2)]
    otm = [A.get([128], BF) for _ in range(2)]
    rden = [A.get([1], F32) for _ in range(2)]
    ub = [A.get([512], BF) for _ in range(2)]
    MEMSET(Vaug[:, :, 128:129], 1.0)
    it = 0
    oi = 0
    for g in range(4):
        load_w(b_w_in, g * 512, 512, wq, stage)
        load_w(b_w_in, W + g * 512, 512, wg, stage)
        for j4 in range(4):
            h = g * 4 + j4
            hs = slice(j4 * 128, (j4 + 1) * 128)
            P.dma("kin0", KT, KTs[h * 128:(h + 1) * 128, :])
            vv = Vs[:, h * 128:(h + 1) * 128].rearrange("(kb p) n -> p kb n", p=128)
            hv = NT // 2
            P.dma("vin0", Vaug[:, 0:hv, 0:128], vv[:, 0:hv, :])
            P.dma("vin1", Vaug[:, hv:NT, 0:128], vv[:, hv:NT, :])
            P.dma("fin0", Fb, Fd[h:h + 1, :].to_broadcast([128, S]))
            TS(nFc, FT[:, :, h], -1.0, ALU.mult)
            for st_ in range(NST):
                b_ = st_ % 2
                ts_ = slice(st_ * 512, (st_ + 1) * 512)
                pq = PS[0 if b_ == 0 else 4]
                pg = PS[1 if b_ == 0 else 5]
                pss = PS[2 if b_ == 0 else 3]
                for kc in range(8):
                    MM(pq[:, :], wq[:, kc, hs], hT[:, kc, ts_], start=(kc == 0), stop=(kc == 7))
                for kc in range(8):
                    MM(pg[:, :], wg[:, kc, hs], hT[:, kc, ts_], start=(kc == 0), stop=(kc == 7))
                ACT(sqb[b_], pq[:, :], AF.Square)
                MM(pss[:, :], ones_b, sqb[b_])
                ACT(T1[b_], pss[:, :], AF.Ln, bias=epsc, scale=1.0 / 128.0)
                ACT(T1[b_], T1[b_], AF.Exp, scale=-0.5)
                STT(QT[:, ts_], pq[:, :], qgs[:, 0:1], T1[b_], ALU.mult, ALU.mult)
                ACT(E1[b_], pg[:, :], AF.Exp, scale=-1.0)
                ACT(E1[b_], E1[b_], AF.Ln, bias=ones_f[:, 0:1])
                ACT(E1[b_], E1[b_], AF.Exp, scale=-1.0)
                TT(sgT[:, ts_], pg[:, :], E1[b_], ALU.mult)
            items = [(qs, kb) for qs in range(NST) for kb in range(4 * qs + 4)]
            accs = [PS[6 + jq // 2][:, (jq % 2) * 256:(jq % 2) * 256 + 129] for jq in range(4)]
            LOOK = 4
            STB = (4, 5, 0, 1)
            pend = {}

            def front(qs, kb):
                nonlocal it
                j0 = max(0, kb - 4 * qs)
                n = 512 - 128 * j0
                qcols = slice(qs * 512 + j0 * 128, (qs + 1) * 512)
                b2 = it % 4
                b3 = it % 6
                it += 1
                pst = PS[STB[b2]][:, 0:n]
                MM(pst, KT[:, kb * 128:(kb + 1) * 128], QT[:, qcols])
                tmp = TMP[b3][:, 0:n]
                TT(tmp, pst, Fb[:, qcols], ALU.add)
                if kb >= 4 * qs:
                    TT(tmp[:, 0:128], tmp[:, 0:128], cbias, ALU.add)
                pt_ = PT[b3][:, 0:n]
                ACT(pt_, tmp, AF.Exp, bias=nFc[:, kb:kb + 1])
                pend[(qs, kb)] = (pt_, j0)

            def back(qs, kb):
                nonlocal oi
                pt_, j0 = pend.pop((qs, kb))
                for jq in range(j0, 4):
                    MM(accs[jq], pt_[:, (jq - j0) * 128:(jq - j0 + 1) * 128], Vaug[:, kb, 0:129],
                       start=(kb == 0 and jq % 2 == 0), stop=(kb == 4 * qs + jq), skip=True)
                if kb == 4 * qs + 3:
                    ub_ = ub[qs % 2]
                    for jq in range(4):
                        o_ = oi % 2
                        oi += 1
                        RECIP(rden[o_], accs[jq][:, 128:129])
                        TS(otm[o_], accs[jq][:, 0:128], rden[o_][:, 0:1], ALU.mult)
                        ptr = PS[3][:, 0:256].bitcast(BF)[:, jq * 128:(jq + 1) * 128]
                        TR(ptr, otm[o_], ident_b)
                        TT(ub_[:, jq * 128:(jq + 1) * 128], ptr,
                           sgT[:, qs * 512 + jq * 128:qs * 512 + (jq + 1) * 128], ALU.mult)
                    DMA("ust", uT[h * 128:(h + 1) * 128, qs * 512:(qs + 1) * 512], ub_, nrot=4)

            for idx in range(len(items) + LOOK):
                if idx < len(items):
                    front(*items[idx])
                if idx - LOOK >= 0:
                    back(*items[idx - LOOK])

    outs = out_phase(b_w_out, x1, mod1[:, 16:24], outp)
    P.emit(final_waits=outs)
    es.close()
    return nc, P


def _prep_inputs(inp, b, S):
    f = lambda a: np.ascontiguousarray(a, dtype=np.float32)
    return {
        "x": f(inp["x"][b, :S]),
        "c_col": f(inp["c"][b].reshape(8, 128).T),
        "mod_w0": f(inp["mod_w"][0]),
        "mod_w1": f(inp["mod_w"][1]),
        "modb_col": f(inp["mod_b"].reshape(2, 24, 128).transpose(2, 0, 1).reshape(128, 48)),
        "ng_col": f(inp["norm_g"].reshape(2, 8, 128).transpose(2, 0, 1).reshape(128, 16)),
        "a_w_in": f(inp["a_w_in"][0]),
        "alb_col": f(inp["a_lb_logits"].reshape(2, 16, 128).transpose(2, 0, 1).reshape(128, 32)),
        "aon_col": f(inp["a_onorm_g"][0].reshape(16, 128).T),
        "a_w_out": f(inp["a_w_out"][0]),
        "kv_mod_w": f(inp["kv_mod_w"]),
        "kvmb_col": f(inp["kv_mod_b"].reshape(16, 128).T),
        "kvng_col": f(inp["kv_norm_g"].reshape(8, 128).T),
        "kv_w": f(inp["kv_w"]),
        "kv_fb": f(inp["kv_fb"].reshape(16, 1)),
        "kng": f(inp["k_norm_g"].reshape(128, 1)),
        "b_w_in": f(inp["b_w_in"][0]),
        "qng": f(inp["b_q_norm_g"][0].reshape(128, 1)),
        "b_w_out": f(inp["b_w_out"][0]),
    }


_CACHE = {}


def kernel(**inputs):
    inp = {k: np.asarray(v) for k, v in inputs.items()}
    S = inp["x"].shape[1]
    if S not in _CACHE:
        _CACHE[S] = build(S)[0]
    nc = _CACHE[S]
    in_maps = [_prep_inputs(inp, b, S) for b in range(8)]
    res = run_bass_kernel_spmd(nc, in_maps, core_ids=list(range(8)))
    return np.stack([np.asarray(r["out"], dtype=np.float32) for r in res.results], axis=0)
```

```python
import bisect
import math
from contextlib import ExitStack

import numpy as np
import concourse.bass as bass
import concourse.mybir as mybir
from concourse.bass_utils import run_bass_kernel_spmd

F32 = mybir.dt.float32
BF = mybir.dt.bfloat16
AF = mybir.ActivationFunctionType
ALU = mybir.AluOpType

D = 1024
W = 2048
H = 16
EPS = 1e-6
NEG = -30000.0

EPOCH = 2000
DMA_EPOCH = 120


def _region(ap):
    pat = ap.ap
    off = ap.offset
    name = ap.name
    if str(ap.space) == "DRAM":
        hi = off
        for st, cnt in pat:
            hi += st * (cnt - 1)
        return (name, 0, 1, off, hi + 1)
    if str(ap.space) == "PSUM":
        return (name, 0, 128, 0, 1 << 40, True)
    pstep, pcnt = pat[0]
    if pstep == 0:
        return (name, 0, 128, 0, 1 << 40)
    p0 = off // pstep
    f0 = off - p0 * pstep
    f1 = f0
    for st, cnt in pat[1:]:
        f1 += st * (cnt - 1)
    es = mybir.dt.size(ap.dtype)
    return (name, p0, p0 + pcnt, f0 * es, (f1 + 1) * es)


def _overlap(a, b):
    return a[1] < b[2] and b[1] < a[2] and a[3] < b[4] and b[3] < a[4]


def _covers(big, small):
    return big[1] <= small[1] and big[2] >= small[2] and big[3] <= small[3] and big[4] >= small[4]


class Op:
    __slots__ = ("eng", "fn", "waits", "need_inc", "dma_key", "stream", "sidx")


class Prog:
    ENGS = ("pe", "act", "dve", "pool", "sp")

    def __init__(self, nc):
        self.nc = nc
        self.q = {e: [] for e in self.ENGS}
        self.stream_ops = {}
        self.known = {e: {} for e in self.ENGS}
        self.ckpt = {}
        self.tw = {}
        self.tr = {}
        self.kver = {e: 0 for e in self.ENGS}
        self.ckver = {}

    def _add(self, queue, fn, reads, writes, dma_key=None):
        op = Op()
        op.eng = queue
        op.fn = fn
        op.dma_key = dma_key
        op.need_inc = dma_key is not None
        stream = dma_key if dma_key is not None else queue
        op.stream = stream
        lst = self.stream_ops.setdefault(stream, [])
        op.sidx = len(lst)
        lst.append(op)
        deps = {}

        def add_dep(o, kind):
            if o.stream == stream and dma_key is None:
                if queue == "pe":
                    return
            if deps.get(o.stream, -1) < o.sidx:
                deps[o.stream] = o.sidx

        rregs = [_region(a) for a in reads]
        wregs = [_region(a) for a in writes]
        if dma_key is not None and op.sidx > 0:
            deps[stream] = op.sidx - 1
        for r in rregs:
            for (wr, wo) in self.tw.get(r[0], ()):
                if _overlap(wr, r):
                    add_dep(wo, "raw")
            if len(r) > 5:
                for (rr, ro) in self.tr.get(r[0], ()):
                    if ro.stream != stream:
                        add_dep(ro, "rar")
        for w in wregs:
            for (wr, wo) in self.tw.get(w[0], ()):
                if _overlap(wr, w):
                    add_dep(wo, "waw")
            for (rr, ro) in self.tr.get(w[0], ()):
                if _overlap(rr, w):
                    add_dep(ro, "war")
        for w in wregs:
            tw = self.tw.setdefault(w[0], [])
            tw[:] = [x for x in tw if not _covers(w, x[0])]
            tw.append((w, op))
            tr = self.tr.get(w[0])
            if tr:
                tr[:] = [x for x in tr if not _covers(w, x[0])]
        for r in rregs:
            tr = self.tr.setdefault(r[0], [])
            tr[:] = [x for x in tr if not (x[1].stream == stream and _covers(r, x[0]))]
            tr.append((r, op))
        kn = self.known[queue]
        waits = [(s, i) for s, i in deps.items() if kn.get(s, -1) < i]
        for s, i in waits:
            self.stream_ops[s][i].need_inc = True
            if kn.get(s, -1) < i:
                kn[s] = i
            ck = self.ckpt.get(s)
            if ck:
                j = bisect.bisect_right(ck[0], i) - 1
                if j >= 0:
                    for s2, i2 in ck[1][j].items():
                        if kn.get(s2, -1) < i2:
                            kn[s2] = i2
        op.waits = waits
        if waits:
            self.kver[queue] += 1
        if self.ckver.get(stream) != self.kver[queue]:
            self.ckver[stream] = self.kver[queue]
            ck = self.ckpt.setdefault(stream, ([], []))
            snap = dict(kn)
            if dma_key is None:
                snap[stream] = op.sidx - 1
            ck[0].append(op.sidx)
            ck[1].append(snap)
        self.q[queue].append(op)
        return op

    def dma(self, key, out, in_, queue="sp"):
        return self._add(queue, lambda e: e.dma_start(out=out, in_=in_), [in_], [out], dma_key=key)

    def emit(self, final_waits=()):
        nc = self.nc
        sem_of = {}
        val_of = {}
        stack = ExitStack()
        nsem = 0
        for stream, ops in self.stream_ops.items():
            is_dma = ops[0].dma_key is not None
            ep_len = DMA_EPOCH if is_dma else EPOCH
            step = 16 if is_dma else 1
            cnt = 0
            for o in ops:
                if not o.need_inc:
                    continue
                ep = cnt // ep_len
                if (stream, ep) not in sem_of:
                    sem_of[(stream, ep)] = stack.enter_context(nc.semaphore("s%d" % nsem))
                    nsem += 1
                val_of[o] = (sem_of[(stream, ep)], (cnt % ep_len + 1) * step, step)
                cnt += 1
        self.nsem = nsem
        engines = {"pe": "tensor", "act": "scalar", "dve": "vector", "pool": "gpsimd", "sp": "sync"}
        fw = [val_of[o] for o in final_waits]
        with stack:
            with nc.Block() as block:
                for qn in self.ENGS:
                    ops = self.q[qn]
                    if not ops:
                        continue

                    def body(eng, ops=ops, qn=qn):
                        for o in ops:
                            for (s, i) in o.waits:
                                sem, val, _ = val_of[self.stream_ops[s][i]]
                                eng.wait_ge(sem, val)
                            inst = o.fn(eng)
                            if o.need_inc:
                                sem, val, step = val_of[o]
                                inst.then_inc(sem, step)
                        if qn == "sp":
                            for (sem, val, _) in fw:
                                eng.wait_ge(sem, val)

                    getattr(block, engines[qn])(body)


class Arena:
    def __init__(self, ap_bf, nbytes):
        self.ap = ap_bf
        self.cap = nbytes
        self.off = 0

    def get(self, shape, dt, parts=128):
        n = 1
        for s in shape:
            n *= s
        nb = n * mybir.dt.size(dt)
        a = self.ap[0:parts, self.off // 2:(self.off + nb) // 2]
        if dt != BF:
            a = a.bitcast(dt)
        if len(shape) == 2:
            a = a.rearrange("p (a b) -> p a b", b=shape[1])
        elif len(shape) == 3:
            a = a.rearrange("p (a b c) -> p a b c", b=shape[1], c=shape[2])
        self.off += (nb + 63) // 64 * 64
        assert self.off <= self.cap, (self.off, self.cap)
        return a


def build(S=4096, dbg=False, stop=None):
    NT = S // 128
    NST = S // 512
    nc = bass.Bass("TRN2", target_bir_lowering=False)
    es = ExitStack()
    es.enter_context(nc.allow_low_precision("bf16 matmul operands, fp32 accumulation"))

    def din(name, shape):
        return nc.dram_tensor(name, shape, F32, kind="ExternalInput").ap()

    x = din("x", [S, D])
    c_col = din("c_col", [128, 8])
    mod_w0 = din("mod_w0", [D, 3 * D])
    mod_w1 = din("mod_w1", [D, 3 * D])
    modb_col = din("modb_col", [128, 48])
    ng_col = din("ng_col", [128, 16])
    a_w_in = din("a_w_in", [D, 4 * W])
    alb_col = din("alb_col", [128, 32])
    aon_col = din("aon_col", [128, 16])
    a_w_out = din("a_w_out", [W, D])
    kv_mod_w = din("kv_mod_w", [D, 2 * D])
    kvmb_col = din("kvmb_col", [128, 16])
    kvng_col = din("kvng_col", [128, 8])
    kv_w = din("kv_w", [D, 2 * W + H])
    kv_fb = din("kv_fb", [H, 1])
    kng = din("kng", [128, 1])
    b_w_in = din("b_w_in", [D, 2 * W])
    qng = din("qng", [128, 1])
    b_w_out = din("b_w_out", [W, D])
    outp = nc.dram_tensor("out", [S, D], F32, kind="ExternalOutput").ap()
    dk = "ExternalOutput" if dbg else "Internal"
    uT = nc.dram_tensor("uT", [W, S], BF, kind="Internal").ap()
    x1 = nc.dram_tensor("x1", [S, D], F32, kind=dk).ap()
    KTs = nc.dram_tensor("KTs", [W, S], BF, kind="Internal").ap()
    Vs = nc.dram_tensor("Vs", [S, W], BF, kind="Internal").ap()

    P = Prog(nc)
    ARENA = 124 * 1024
    arena_t = es.enter_context(nc.sbuf_tensor("arena", [128, ARENA // 2], BF))
    hT = es.enter_context(nc.sbuf_tensor("hT", [128, 8, S], BF))
    cst = es.enter_context(nc.sbuf_tensor("cst", [128, 2560], F32))
    FT = es.enter_context(nc.sbuf_tensor("FT", [128, NT, 16], F32))
    Fd = nc.dram_tensor("Fd", [H, S], F32, kind="Internal").ap()
    PS = [es.enter_context(nc.psum_tensor("ps%d" % i, [128, 512], F32)) for i in range(8)]
    A = Arena(arena_t[:], ARENA)
    C = Arena(cst[:].bitcast(BF), 10240)

    def aps(*xs):
        return [a for a in xs if a is not None and not isinstance(a, (int, float))]

    def MM(out, lhsT, rhs, start=True, stop=True, skip=False):
        if skip:
            P._add("pe", lambda e: e.matmul(out, lhsT, rhs, start=start, stop=stop, skip_group_check=True),
                   [lhsT, rhs], [out])
        else:
            P._add("pe", lambda e: e.matmul(out, lhsT, rhs, start=start, stop=stop), [lhsT, rhs], [out])

    def TR(out, in_, ident):
        P._add("pe", lambda e: e.transpose(out, in_, ident), [in_, ident], [out])

    def ACT(out, in_, func, bias=0.0, scale=1.0, accum_out=None):
        P._add("act", lambda e: e.activation(out=out, in_=in_, func=func, bias=bias, scale=scale,
                                             accum_out=accum_out),
               aps(in_, bias, scale), aps(out, accum_out))

    def TT(out, in0, in1, op, eng="dve"):
        P._add(eng, lambda e: e.tensor_tensor(out=out, in0=in0, in1=in1, op=op), [in0, in1], [out])

    def TS(out, in0, s1, op0, s2=None, op1=None, eng="dve"):
        if op1 is None:
            P._add(eng, lambda e: e.tensor_scalar(out=out, in0=in0, scalar1=s1, scalar2=None, op0=op0),
                   aps(in0, s1), [out])
        else:
            P._add(eng, lambda e: e.tensor_scalar(out=out, in0=in0, scalar1=s1, scalar2=s2, op0=op0, op1=op1),
                   aps(in0, s1, s2), [out])

    def STT(out, in0, scalar, in1, op0, op1, eng="dve"):
        P._add(eng, lambda e: e.scalar_tensor_tensor(out=out, in0=in0, scalar=scalar, in1=in1, op0=op0, op1=op1),
               aps(in0, scalar, in1), [out])

    def COPY(out, in_, eng="dve"):
        if eng == "act":
            P._add("act", lambda e: e.copy(out=out, in_=in_), [in_], [out])
        else:
            P._add(eng, lambda e: e.tensor_copy(out=out, in_=in_), [in_], [out])

    def RECIP(out, in_):
        P._add("dve", lambda e: e.reciprocal(out=out, in_=in_), [in_], [out])

    def MEMSET(ap, v, eng="pool"):
        P._add(eng, lambda e: e.memset(ap, v), [], [ap])

    def ASEL(out, in_, pattern, op, fill, base, cm):
        P._add("pool", lambda e: e.affine_select(out=out, in_=in_, pattern=pattern, compare_op=op, fill=fill,
                                                 base=base, channel_multiplier=cm), [in_], [out])

    dcount = {}

    def DMA(cls, out, in_, nrot=1):
        k = dcount.get(cls, 0)
        dcount[cls] = k + 1
        return P.dma("%s%d" % (cls, k % nrot), out, in_)

    ident_f = C.get([128], F32)
    ident_b = C.get([128], BF)
    ones_b = C.get([128], BF)
    ones_f = C.get([128], F32)
    cmask = C.get([512], F32)
    amask = C.get([128], F32)
    cbias = C.get([128], F32)
    c_sb = C.get([8], F32)
    sc_sb = C.get([8], F32)
    modb = C.get([48], F32)
    ngc = C.get([16], F32)
    albc = C.get([32], F32)
    aonc = C.get([16], F32)
    kvmbc = C.get([16], F32)
    kvngc = C.get([8], F32)
    kngc = C.get([1], F32)
    qngc = C.get([1], F32)
    kvfbc = C.get([1], F32, parts=16)
    mod0 = C.get([24], F32)
    mod1 = C.get([24], F32)
    kvmod = C.get([16], F32)
    A0 = C.get([8], F32)
    A1 = C.get([8], F32)
    Akv = C.get([8], F32)
    lbc = C.get([16], F32)
    omlc = C.get([16], F32)
    nomlc = C.get([16], F32)
    qgs = C.get([1], F32)
    tmpc = C.get([32], F32)
    epsc = C.get([1], F32)
    ones512 = C.get([512], F32)

    MEMSET(epsc, EPS)
    MEMSET(ones512, 1.0)
    MEMSET(ident_f, 0.0)
    ASEL(ident_f, ident_f, [[-1, 128]], ALU.not_equal, 1.0, 0, 1)
    COPY(ident_b, ident_f)
    MEMSET(ones_b, 1.0)
    MEMSET(ones_f, 1.0)
    MEMSET(cmask, 1.0)
    MEMSET(cmask.rearrange("p (c t) -> p c t", t=64)[:, :, 0:1], 0.0)
    MEMSET(amask, 1.0)
    ASEL(amask, amask, [[1, 128]], ALU.is_ge, 0.0, 0, -1)
    MEMSET(amask[0:64, 64:128], 0.0)
    MEMSET(cbias, 0.0)
    ASEL(cbias, cbias, [[1, 128]], ALU.is_ge, NEG, 0, -1)
    for i, (dst, src) in enumerate([(c_sb, c_col), (modb, modb_col), (ngc, ng_col), (albc, alb_col),
                                    (aonc, aon_col), (kvmbc, kvmb_col), (kvngc, kvng_col), (kngc, kng),
                                    (qngc, qng), (kvfbc, kv_fb)]):
        P.dma("cst%d" % i, dst, src)

    ACT(sc_sb, c_sb, AF.Silu)

    def modcalc(wsrc, ncol, bcol, dst):
        A.off = 0
        nch = ncol // 128
        wt = A.get([8, ncol], F32)
        wv = wsrc.rearrange("(kc p) n -> p kc n", p=128)
        for kc in range(8):
            P.dma("mw%d" % (kc % 4), wt[:, kc, :], wv[:, kc, :])
        ps = PS[0]
        for m in range(nch):
            for kc in range(8):
                MM(ps[:, m:m + 1], wt[:, kc, m * 128:(m + 1) * 128], sc_sb[:, kc:kc + 1],
                   start=(kc == 0), stop=(kc == 7))
        TT(dst, ps[:, 0:nch], bcol, ALU.add)

    def modcalc_bg(wsrc, ncol, bcol, dst, off):
        wv = wsrc.rearrange("(kc p) n -> p kc n", p=128)
        save = A.off
        A.off = off
        regs = [A.get([8, 512], F32) for _ in range(2)]
        A.off = save
        ps = PS[0]
        for ci, c0 in enumerate(range(0, ncol, 512)):
            wt = regs[ci % 2]
            for half in range(2):
                P.dma("mwb%d" % ((2 * ci + half) % 4), wt[:, half * 4:(half + 1) * 4, :],
                      wv[:, half * 4:(half + 1) * 4, c0:c0 + 512], queue="pool")
            yield
            for m in range(4):
                mg = c0 // 128 + m
                for kc in range(8):
                    MM(ps[:, mg:mg + 1], wt[:, kc, m * 128:(m + 1) * 128], sc_sb[:, kc:kc + 1],
                       start=(kc == 0), stop=(kc == 7))
            yield
        TT(dst, ps[:, 0:ncol // 128], bcol, ALU.add)
        yield

    modcalc(mod_w0, 3 * D, modb[:, 0:24], mod0)
    STT(A0, mod0[:, 8:16], 1.0, ngc[:, 0:8], ALU.add, ALU.mult)
    TT(tmpc[:, 0:16], albc[:, 0:16], albc[:, 16:32], ALU.subtract)
    ACT(lbc, tmpc[:, 0:16], AF.Sigmoid)
    ACT(omlc, tmpc[:, 0:16], AF.Sigmoid, scale=-1.0)
    TS(nomlc, omlc, -1.0, ALU.mult)
    TS(qgs, qngc, 1.0 / math.sqrt(128.0), ALU.mult)

    def prepass_p1(xt, ss, xn):
        ACT(prepass_p1.junk, xt, AF.Square, accum_out=ss[:, 0:1])
        ACT(ss[:, 1:2], ss[:, 0:1], AF.Ln, bias=epsc, scale=1.0 / D)
        ACT(ss[:, 2:3], ss[:, 1:2], AF.Exp, scale=-0.5)
        TS(xn, xt, ss[:, 2:3], ALU.mult)

    def prepass_p2(xn, tt, Acol, shcol, pbank):
        for kc in range(8):
            pb = PS[pbank + kc // 4][:, 0:256].bitcast(BF)[:, (kc % 4) * 128:(kc % 4 + 1) * 128]
            TR(pb, xn[:, kc * 128:(kc + 1) * 128], ident_b)
        for kc in range(8):
            pb = PS[pbank + kc // 4][:, 0:256].bitcast(BF)[:, (kc % 4) * 128:(kc % 4 + 1) * 128]
            if kc < 4:
                ACT(hT[:, kc, tt * 128:(tt + 1) * 128], pb, AF.Identity, bias=shcol[:, kc:kc + 1],
                    scale=Acol[:, kc:kc + 1])
            else:
                TS(hT[:, kc, tt * 128:(tt + 1) * 128], pb, Acol[:, kc:kc + 1], ALU.mult,
                   shcol[:, kc:kc + 1], ALU.add)

    def prepass_from_dram(src, Acol, shcol, bg=None):
        A.off = 0
        xts = [A.get([D], F32) for _ in range(3)]
        prepass_p1.junk = A.get([D], F32)
        xns = [A.get([D], BF) for _ in range(2)]
        sss = [A.get([4], F32) for _ in range(2)]
        for tt in range(NT):
            xt = xts[tt % 3]
            DMA("xin", xt, src[tt * 128:(tt + 1) * 128, :], nrot=2)
            prepass_p1(xt, sss[tt % 2], xns[tt % 2])
            if tt > 0:
                prepass_p2(xns[(tt - 1) % 2], tt - 1, Acol, shcol, 4 + 2 * ((tt - 1) % 2))
            if bg is not None and next(bg, "done") == "done":
                bg = None
        prepass_p2(xns[(NT - 1) % 2], NT - 1, Acol, shcol, 4 + 2 * ((NT - 1) % 2))
        if bg is not None:
            for _ in bg:
                pass

    def load_w(src2d, c0, ncols, dst, stage):
        sv = src2d.rearrange("(kc p) n -> p kc n", p=128)
        for half in range(2):
            k = load_w.n
            load_w.n += 1
            st = stage[k % 2]
            P.dma("wst%d" % (k % 2), st[:, :, 0:ncols], sv[:, half * 4:(half + 1) * 4, c0:c0 + ncols])
            COPY(dst[:, half * 4:(half + 1) * 4, :], st[:, :, 0:ncols], eng="pool" if half == 0 else "act")
    load_w.n = 0

    def out_phase(w_out, xsrc, gcol, dst, next_pre=None):
        A.off = 0
        wo = A.get([16, D], BF)
        stage = [A.get([4, 512], F32) for _ in range(2)]
        wov = w_out.rearrange("(kc p) n -> p kc n", p=128)
        k = 0
        for q4 in range(4):
            for nh in range(2):
                st = stage[k % 2]
                P.dma("wst%d" % (k % 2), st, wov[:, q4 * 4:(q4 + 1) * 4, nh * 512:(nh + 1) * 512])
                COPY(wo[:, q4 * 4:(q4 + 1) * 4, nh * 512:(nh + 1) * 512], st, eng="pool")
                k += 1
        grow = A.get([D], F32)
        dg = A.get([128], F32)
        for kc in range(8):
            TS(dg, ident_f, gcol[:, kc:kc + 1], ALU.mult)
            pb = PS[0][:, 0:128]
            MM(pb, ones_f, dg)
            COPY(grow[:, kc * 128:(kc + 1) * 128], pb)
        uts = [A.get([16, 512], BF) for _ in range(2)]
        xts = [A.get([D], F32) for _ in range(2)]
        xos = [A.get([D], F32) for _ in range(3)]
        prepass_p1.junk = A.get([D], F32)
        xns = [A.get([D], BF) for _ in range(2)]
        sss = [A.get([4], F32) for _ in range(2)]
        uv = uT.rearrange("(kc p) s -> p kc s", p=128)
        outs = []
        for st_ in range(NST):
            ut = uts[st_ % 2]
            DMA("uin", ut, uv[:, :, st_ * 512:(st_ + 1) * 512], nrot=2)
            for t4 in range(4):
                tt = st_ * 4 + t4
                xt = xts[tt % 2]
                xo = xos[tt % 3]
                DMA("xin", xt, xsrc[tt * 128:(tt + 1) * 128, :], nrot=2)
                pb0 = 2 * (tt % 2)
                for nh in range(2):
                    for kc in range(16):
                        MM(PS[pb0 + nh][:, :], ut[:, kc, t4 * 128:(t4 + 1) * 128],
                           wo[:, kc, nh * 512:(nh + 1) * 512], start=(kc == 0), stop=(kc == 15))
                for nh in range(2):
                    sl = slice(nh * 512, (nh + 1) * 512)
                    TT(xo[:, sl], PS[pb0 + nh][:, :], grow[:, sl], ALU.mult)
                    TT(xo[:, sl], xo[:, sl], xt[:, sl], ALU.add, eng="pool")
                outs.append(DMA("xst", dst[tt * 128:(tt + 1) * 128, :], xo, nrot=4))
                if next_pre is not None:
                    prepass_p1(xo, sss[tt % 2], xns[tt % 2])
                    if tt > 0:
                        prepass_p2(xns[(tt - 1) % 2], tt - 1, next_pre[0], next_pre[1], 4 + 2 * ((tt - 1) % 2))
        if next_pre is not None:
            prepass_p2(xns[(NT - 1) % 2], NT - 1, next_pre[0], next_pre[1], 4 + 2 * ((NT - 1) % 2))
        return outs

    def _stop():
        P.emit()
        es.close()
        return nc, P

    if stop == "mod":
        return _stop()
    def _bg_mods():
        yield from modcalc_bg(kv_mod_w, 2 * D, kvmbc, kvmod, 24 * 1024)
        STT(Akv, kvmod[:, 8:16], 1.0, kvngc, ALU.add, ALU.mult)
        yield from modcalc_bg(mod_w1, 3 * D, modb[:, 24:48], mod1, 24 * 1024)
        STT(A1, mod1[:, 8:16], 1.0, ngc[:, 8:16], ALU.add, ALU.mult)

    prepass_from_dram(x, A0, mod0[:, 0:8], bg=_bg_mods())
    if stop == "pre":
        return _stop()

    A.off = 0
    stage = [A.get([4, 512], F32) for _ in range(2)]
    wq = A.get([8, 512], BF)
    wf = A.get([8, 512], BF)
    wi = A.get([8, 512], BF)
    wg = A.get([8, 512], BF)
    vg = A.get([NT, 512], BF)
    NB = 2
    T1 = [A.get([512], F32) for _ in range(NB)]
    T2 = [A.get([512], F32) for _ in range(NB)]
    T3 = [A.get([512], F32) for _ in range(NB)]
    T4 = [A.get([512], F32) for _ in range(NB)]
    T5 = [A.get([512], F32) for _ in range(NB)]
    qdec = [A.get([512], BF) for _ in range(NB)]
    kinb = [A.get([512], BF) for _ in range(NB)]
    kstb = [A.get([512], BF) for _ in range(NB)]
    sgb = [A.get([512], BF) for _ in range(NB)]
    sqb = [A.get([512], BF) for _ in range(NB)]
    ub = [A.get([512], BF) for _ in range(NB)]
    kstT = [A.get([4, 128], BF) for _ in range(NB)]
    attb = [A.get([128], BF) for _ in range(4)]
    S32 = [A.get([128], F32) for _ in range(2)]
    Sbf = [A.get([128], BF) for _ in range(2)]

    def stageA(h, j, st_, b_):
        hs = slice(j * 128, (j + 1) * 128)
        ts_ = slice(st_ * 512, (st_ + 1) * 512)
        for (pb, wt) in ((PS[1], wf), (PS[2], wg), (PS[0], wq)):
            for kc in range(8):
                MM(pb[:, :], wt[:, kc, hs], hT[:, kc, ts_], start=(kc == 0), stop=(kc == 7))
            yield
        t1, t2, t3, t4_, t5 = T1[b_], T2[b_], T3[b_], T4[b_], T5[b_]
        ACT(t1, PS[1][:, :], AF.Sigmoid)
        ACT(sqb[b_], PS[2][:, :], AF.Sigmoid)
        yield
        TS(t2, t1, nomlc[:, h:h + 1], ALU.mult, omlc[:, h:h + 1], ALU.add, eng="pool")
        TT(sgb[b_], PS[2][:, :], sqb[b_], ALU.mult)
        COPY(t5, PS[0][:, :], eng="act")
        yield
        ACT(t1, t1, AF.Ln, bias=lbc[:, h:h + 1], scale=omlc[:, h:h + 1])
        yield
        P._add("dve", lambda e, o=t3, d1=t1: e.tensor_tensor_scan(out=o, data0=cmask, data1=d1, initial=0.0,
                                                                  op0=ALU.mult, op1=ALU.add),
               [cmask, t1], [t3])
        yield
        ACT(t4_, t3, AF.Exp)
        ACT(t1, t3, AF.Exp, scale=-1.0)
        yield
        TT(qdec[b_], t5, t4_, ALU.mult, eng="pool")
        TT(t2, t2, t1, ALU.mult)
        yield
        COPY(kinb[b_], t2, eng="pool")
        eb3 = t4_.rearrange("p (c t) -> p c t", t=64)
        TT(kstb[b_].rearrange("p (c t) -> p c t", t=64), t2.rearrange("p (c t) -> p c t", t=64),
           eb3[:, :, 63:64].to_broadcast([128, 8, 64]), ALU.mult, eng="pool")
        yield
        ptr = PS[6][:, 0:256].bitcast(BF)
        for q4 in range(4):
            TR(ptr[:, q4 * 128:(q4 + 1) * 128], kstb[b_][:, q4 * 128:(q4 + 1) * 128], ident_b)
        COPY(kstT[b_].rearrange("p a b -> p (a b)"), ptr, eng="act")
        yield

    sidx_box = [0]

    def stageB(h, j, st_, b_):
        hs = slice(j * 128, (j + 1) * 128)
        ts_ = slice(st_ * 512, (st_ + 1) * 512)
        t1, t2, t4_ = T1[b_], T2[b_], T4[b_]
        po = PS[3]
        if st_ == 0:
            sidx_box[0] = 0
        for q4 in range(4):
            tt = st_ * 4 + q4
            cs = slice(q4 * 128, (q4 + 1) * 128)
            pa = PS[4][:, 0:128]
            MM(pa, kinb[b_][:, cs], qdec[b_][:, cs])
            ab = attb[q4]
            TT(ab, pa, amask, ALU.mult)
            for jj in range(2):
                sidx = sidx_box[0]
                n = st_ * 8 + q4 * 2 + jj
                rs = slice(jj * 64, (jj + 1) * 64)
                oc = slice(q4 * 128 + jj * 64, q4 * 128 + (jj + 1) * 64)
                first = (n == 0)
                pu = PS[5 if n % 2 == 0 else 7][:, 0:128]
                MM(pu, kstT[b_][rs, q4, :], vg[rs, tt, hs])
                MM(po[:, oc], vg[rs, tt, hs], ab[rs, rs], start=True, stop=first)
                if not first:
                    MM(po[:, oc], Sbf[sidx % 2], qdec[b_][:, oc], start=False, stop=True)
                dcol = t4_[:, q4 * 128 + jj * 64 + 63:q4 * 128 + jj * 64 + 64]
                if first:
                    COPY(Sbf[(sidx + 1) % 2], pu)
                    COPY(S32[(sidx + 1) % 2], pu)
                else:
                    STT(Sbf[(sidx + 1) % 2], S32[sidx % 2], dcol, pu, ALU.mult, ALU.add)
                    STT(S32[(sidx + 1) % 2], S32[sidx % 2], dcol, pu, ALU.mult, ALU.add)
                sidx_box[0] = sidx + 1
                yield
        ACT(sqb[b_], po[:, :], AF.Square)
        MM(PS[6][:, :], ones_b, sqb[b_])
        yield
        ACT(t1, PS[6][:, :], AF.Ln, bias=epsc, scale=1.0 / 128.0)
        ACT(t1, t1, AF.Exp, scale=-0.5)
        yield
        TT(t2, po[:, :], t1, ALU.mult)
        STT(ub[b_], t2, aonc[:, h:h + 1], sgb[b_], ALU.mult, ALU.mult)
        DMA("ust", uT[h * 128:(h + 1) * 128, ts_], ub[b_], nrot=4)
        yield

    def interleave(ga, gb, pre=3):
        done_a = ga is None
        done_b = gb is None
        if not done_a:
            for _ in range(pre):
                try:
                    next(ga)
                except StopIteration:
                    done_a = True
                    break
        while not (done_a and done_b):
            if not done_b:
                try:
                    next(gb)
                except StopIteration:
                    done_b = True
            if not done_a:
                try:
                    next(ga)
                except StopIteration:
                    done_a = True

    hb = 0
    for g in range(4):
        load_w(a_w_in, 0 * W + g * 512, 512, wq, stage)
        load_w(a_w_in, 1 * W + g * 512, 512, wf, stage)
        load_w(a_w_in, 2 * W + g * 512, 512, wi, stage)
        load_w(a_w_in, 3 * W + g * 512, 512, wg, stage)
        for tt in range(NT):
            pv = PS[6 + tt % 2]
            for kc in range(8):
                MM(pv[:, :], hT[:, kc, tt * 128:(tt + 1) * 128], wi[:, kc, :], start=(kc == 0), stop=(kc == 7))
            if tt % 2 == 0:
                COPY(vg[:, tt, :], pv[:, :], eng="act")
            else:
                COPY(vg[:, tt, :], pv[:, :])
        tiles = [(g * 4 + j, j, st_) for j in range(4) for st_ in range(NST)]
        prev = None
        for (h, j, st_) in tiles:
            b_ = hb % NB
            hb += 1
            ga = stageA(h, j, st_, b_)
            gb = stageB(*prev) if prev is not None else None
            interleave(ga, gb)
            prev = (h, j, st_, b_)
        interleave(None, stageB(*prev))

    if stop == "l0b":
        return _stop()
    out_phase(a_w_out, x, mod0[:, 16:24], x1, next_pre=(Akv, kvmod[:, 0:8]))

    if stop == "l0c":
        return _stop()
    A.off = 0
    stage = [A.get([4, 512], F32) for _ in range(2)]
    wk = A.get([8, 512], BF)
    wv = A.get([8, 512], BF)
    wfl = A.get([8, 16], BF)
    vst = [A.get([4, 512], BF) for _ in range(2)]
    T1 = [A.get([512], F32) for _ in range(2)]
    sqb = [A.get([512], BF) for _ in range(2)]
    ktb = [A.get([512], BF) for _ in range(2)]
    Fall = A.get([S], F32, parts=16)
    lsg = A.get([512], F32, parts=16)

    sv = kv_w.rearrange("(kc p) n -> p kc n", p=128)
    P.dma("wst0", stage[0][:, :, 0:16], sv[:, 0:4, 2 * W:2 * W + 16])
    COPY(wfl[:, 0:4, :], stage[0][:, :, 0:16], eng="pool")
    P.dma("wst1", stage[1][:, :, 0:16], sv[:, 4:8, 2 * W:2 * W + 16])
    COPY(wfl[:, 4:8, :], stage[1][:, :, 0:16], eng="pool")
    for st_ in range(NST):
        ts_ = slice(st_ * 512, (st_ + 1) * 512)
        pf = PS[4][0:16, :]
        for kc in range(8):
            MM(pf, wfl[:, kc, :], hT[:, kc, ts_], start=(kc == 0), stop=(kc == 7))
        ACT(lsg, pf, AF.Sigmoid, bias=kvfbc)
        ACT(lsg, lsg, AF.Ln)
        init = 0.0 if st_ == 0 else Fall[:, st_ * 512 - 1:st_ * 512]
        P._add("dve", lambda e, o=Fall[:, ts_], ini=init: e.tensor_tensor_scan(
            out=o, data0=ones512[0:16, :], data1=lsg, initial=ini, op0=ALU.mult, op1=ALU.add),
            aps(lsg, init, ones512[0:16, :]), [Fall[:, ts_]])
    P.dma("fst0", Fd, Fall)
    for kb in range(NT):
        pt = PS[5][:, 0:16]
        TR(pt, Fall[:, kb * 128:(kb + 1) * 128], ident_f[0:16, 0:16])
        COPY(FT[:, kb, :], pt)

    for g in range(4):
        load_w(kv_w, g * 512, 512, wk, stage)
        load_w(kv_w, W + g * 512, 512, wv, stage)
        for tt in range(NT):
            pv = PS[2 + tt % 2]
            for kc in range(8):
                MM(pv[:, :], hT[:, kc, tt * 128:(tt + 1) * 128], wv[:, kc, :], start=(kc == 0), stop=(kc == 7))
            vs_ = vst[(tt // 4) % 2]
            if tt % 2 == 0:
                COPY(vs_[:, tt % 4, :], pv[:, :], eng="act")
            else:
                COPY(vs_[:, tt % 4, :], pv[:, :])
            if tt % 4 == 3:
                t0 = tt - 3
                DMA("vst", Vs[t0 * 128:(t0 + 4) * 128, g * 512:(g + 1) * 512].rearrange("(a p) n -> p a n", p=128),
                    vs_, nrot=2)
        for j in range(4):
            h = g * 4 + j
            hs = slice(j * 128, (j + 1) * 128)
            for st_ in range(NST):
                b_ = st_ % 2
                ts_ = slice(st_ * 512, (st_ + 1) * 512)
                pk = PS[0 if b_ == 0 else 5]
                pss = PS[1 if b_ == 0 else 6]
                for kc in range(8):
                    MM(pk[:, :], wk[:, kc, hs], hT[:, kc, ts_], start=(kc == 0), stop=(kc == 7))
                ACT(sqb[b_], pk[:, :], AF.Square)
                MM(pss[:, :], ones_b, sqb[b_])
                ACT(T1[b_], pss[:, :], AF.Ln, bias=epsc, scale=1.0 / 128.0)
                ACT(T1[b_], T1[b_], AF.Exp, scale=-0.5)
                STT(ktb[b_], pk[:, :], kngc[:, 0:1], T1[b_], ALU.mult, ALU.mult)
                DMA("kst", KTs[h * 128:(h + 1) * 128, ts_], ktb[b_], nrot=2)

    if stop == "kv":
        return _stop()
    prepass_from_dram(x1, A1, mod1[:, 0:8])

    A.off = 0
    stage = [A.get([4, 512], F32) for _ in range(2)]
    wq = A.get([8, 512], BF)
    wg = A.get([8, 512], BF)
    KT = A.get([S], BF)
    Vaug = A.get([NT, 130], BF)
    QT = A.get([S], BF)
    sgT = A.get([S], BF)
    Fb = A.get([S], F32)
    nFc = A.get([NT], F32)
    TMP = [A.get([512], F32) for _ in range(6)]
    PT = [A.get([512], BF) for _ in range(6)]
    T1 = [A.get([512], F32) for _ in range(2)]
    sqb = [A.get([512], BF) for _ in range(2)]
    E1 = [A.get([512], F32) for _ in range(2)]
    otm4 = [A.get([128], BF) for _ in range(4)]
    rden4 = [A.get([1], F32) for _ in range(4)]
    accS = [[A.get([385], F32) for _ in range(2)] for _ in range(2)]
    epi_q = []
    epi_n = [0]
    ub = [A.get([512], BF) for _ in range(2)]
    MEMSET(Vaug[:, :, 128:129], 1.0)
    it = 0
    oi = 0
    for g in range(4):
        load_w(b_w_in, g * 512, 512, wq, stage)
        load_w(b_w_in, W + g * 512, 512, wg, stage)
        for j4 in range(4):
            h = g * 4 + j4
            hs = slice(j4 * 128, (j4 + 1) * 128)
            P.dma("kin0", KT, KTs[h * 128:(h + 1) * 128, :])
            vv = Vs[:, h * 128:(h + 1) * 128].rearrange("(kb p) n -> p kb n", p=128)
            hv = NT // 2
            P.dma("vin0", Vaug[:, 0:hv, 0:128], vv[:, 0:hv, :])
            P.dma("vin1", Vaug[:, hv:NT, 0:128], vv[:, hv:NT, :])
            P.dma("fin0", Fb, Fd[h:h + 1, :].to_broadcast([128, S]))
            TS(nFc, FT[:, :, h], -1.0, ALU.mult)
            for st_ in range(NST):
                b_ = st_ % 2
                ts_ = slice(st_ * 512, (st_ + 1) * 512)
                pq = PS[0 if b_ == 0 else 4]
                pg = PS[1 if b_ == 0 else 5]
                pss = PS[2 if b_ == 0 else 3]
                for kc in range(8):
                    MM(pq[:, :], wq[:, kc, hs], hT[:, kc, ts_], start=(kc == 0), stop=(kc == 7))
                for kc in range(8):
                    MM(pg[:, :], wg[:, kc, hs], hT[:, kc, ts_], start=(kc == 0), stop=(kc == 7))
                ACT(sqb[b_], pq[:, :], AF.Square)
                MM(pss[:, :], ones_b, sqb[b_])
                ACT(T1[b_], pss[:, :], AF.Ln, bias=epsc, scale=1.0 / 128.0)
                ACT(T1[b_], T1[b_], AF.Exp, scale=-0.5)
                STT(QT[:, ts_], pq[:, :], qgs[:, 0:1], T1[b_], ALU.mult, ALU.mult)
                ACT(E1[b_], pg[:, :], AF.Exp, scale=-1.0)
                ACT(E1[b_], E1[b_], AF.Ln, bias=ones_f[:, 0:1])
                ACT(E1[b_], E1[b_], AF.Exp, scale=-1.0)
                TT(sgT[:, ts_], pg[:, :], E1[b_], ALU.mult)
            items = [(qs, kb) for qs in range(NST) for kb in range(4 * qs + 4)]
            accs = [PS[6 + jq // 2][:, (jq % 2) * 256:(jq % 2) * 256 + 129] for jq in range(4)]
            LOOK = 4
            STB = (4, 5, 0, 1)
            pend = {}

            def front(qs, kb):
                nonlocal it
                j0 = max(0, kb - 4 * qs)
                n = 512 - 128 * j0
                qcols = slice(qs * 512 + j0 * 128, (qs + 1) * 512)
                b2 = it % 4
                b3 = it % 6
                it += 1
                pst = PS[STB[b2]][:, 0:n]
                MM(pst, KT[:, kb * 128:(kb + 1) * 128], QT[:, qcols])
                tmp = TMP[b3][:, 0:n]
                TT(tmp, pst, Fb[:, qcols], ALU.add)
                if kb >= 4 * qs:
                    TT(tmp[:, 0:128], tmp[:, 0:128], cbias, ALU.add)
                pt_ = PT[b3][:, 0:n]
                ACT(pt_, tmp, AF.Exp, bias=nFc[:, kb:kb + 1])
                pend[(qs, kb)] = (pt_, j0)

            def back(qs, kb):
                nonlocal oi
                pt_, j0 = pend.pop((qs, kb))
                for jq in range(j0, 4):
                    MM(accs[jq], pt_[:, (jq - j0) * 128:(jq - j0 + 1) * 128], Vaug[:, kb, 0:129],
                       start=(kb == 0 and jq % 2 == 0), stop=(kb == 4 * qs + jq), skip=True)
                if kb == 4 * qs + 3:
                    sl = epi_n[0] % 2
                    epi_n[0] += 1
                    for bnk in range(2):
                        COPY(accS[sl][bnk], PS[6 + bnk][:, 0:385])
                    epi_q.append([1, qs, sl, 0])

            def epi2(qs, sl):
                for jq in range(4):
                    src = accS[sl][jq // 2]
                    c0 = (jq % 2) * 256
                    RECIP(rden4[jq], src[:, c0 + 128:c0 + 129])
                    TS(otm4[jq], src[:, c0:c0 + 128], rden4[jq][:, 0:1], ALU.mult)

            def epi3(qs, sl):
                ub_ = ub[qs % 2]
                for jq in range(4):
                    ptr = PS[3][:, 0:256].bitcast(BF)[:, jq * 128:(jq + 1) * 128]
                    TR(ptr, otm4[jq], ident_b)
                for jq in range(4):
                    ptr = PS[3][:, 0:256].bitcast(BF)[:, jq * 128:(jq + 1) * 128]
                    TT(ub_[:, jq * 128:(jq + 1) * 128], ptr,
                       sgT[:, qs * 512 + jq * 128:qs * 512 + (jq + 1) * 128], ALU.mult)
                DMA("ust", uT[h * 128:(h + 1) * 128, qs * 512:(qs + 1) * 512], ub_, nrot=4)

            def epi_step(flush=False):
                for e in list(epi_q):
                    e[3] += 1
                    if e[0] == 1 and (flush or e[3] >= 2):
                        epi2(e[1], e[2])
                        e[0] = 2
                        if not flush:
                            continue
                    if e[0] == 2 and (flush or e[3] >= 4):
                        epi3(e[1], e[2])
                        epi_q.remove(e)

            for idx in range(len(items) + LOOK):
                if idx < len(items):
                    front(*items[idx])
                if idx - LOOK >= 0:
                    back(*items[idx - LOOK])
                epi_step()
            epi_step(flush=True)

    outs = out_phase(b_w_out, x1, mod1[:, 16:24], outp)
    P.emit(final_waits=outs)
    es.close()
    return nc, P


def _prep_inputs(inp, b, S):
    f = lambda a: np.ascontiguousarray(a, dtype=np.float32)
    return {
        "x": f(inp["x"][b, :S]),
        "c_col": f(inp["c"][b].reshape(8, 128).T),
        "mod_w0": f(inp["mod_w"][0]),
        "mod_w1": f(inp["mod_w"][1]),
        "modb_col": f(inp["mod_b"].reshape(2, 24, 128).transpose(2, 0, 1).reshape(128, 48)),
        "ng_col": f(inp["norm_g"].reshape(2, 8, 128).transpose(2, 0, 1).reshape(128, 16)),
        "a_w_in": f(inp["a_w_in"][0]),
        "alb_col": f(inp["a_lb_logits"].reshape(2, 16, 128).transpose(2, 0, 1).reshape(128, 32)),
        "aon_col": f(inp["a_onorm_g"][0].reshape(16, 128).T),
        "a_w_out": f(inp["a_w_out"][0]),
        "kv_mod_w": f(inp["kv_mod_w"]),
        "kvmb_col": f(inp["kv_mod_b"].reshape(16, 128).T),
        "kvng_col": f(inp["kv_norm_g"].reshape(8, 128).T),
        "kv_w": f(inp["kv_w"]),
        "kv_fb": f(inp["kv_fb"].reshape(16, 1)),
        "kng": f(inp["k_norm_g"].reshape(128, 1)),
        "b_w_in": f(inp["b_w_in"][0]),
        "qng": f(inp["b_q_norm_g"][0].reshape(128, 1)),
        "b_w_out": f(inp["b_w_out"][0]),
    }


_CACHE = {}


def kernel(**inputs):
    inp = {k: np.asarray(v) for k, v in inputs.items()}
    S = inp["x"].shape[1]
    if S not in _CACHE:
        _CACHE[S] = build(S)[0]
    nc = _CACHE[S]
    in_maps = [_prep_inputs(inp, b, S) for b in range(8)]
    res = run_bass_kernel_spmd(nc, in_maps, core_ids=list(range(8)))
    return np.stack([np.asarray(r["out"], dtype=np.float32) for r in res.results], axis=0)
```

```python
import bisect
import math
from contextlib import ExitStack

import numpy as np
import concourse.bass as bass
import concourse.mybir as mybir
from concourse.bass_utils import run_bass_kernel_spmd

F32 = mybir.dt.float32
BF = mybir.dt.bfloat16
AF = mybir.ActivationFunctionType
ALU = mybir.AluOpType

D = 1024
W = 2048
H = 16
EPS = 1e-6
NEG = -30000.0

EPOCH = 2000
DMA_EPOCH = 120


def _region(ap):
    pat = ap.ap
    off = ap.offset
    name = ap.name
    if str(ap.space) == "DRAM":
        hi = off
        for st, cnt in pat:
            hi += st * (cnt - 1)
        return (name, 0, 1, off, hi + 1)
    if str(ap.space) == "PSUM":
        return (name, 0, 128, 0, 1 << 40, True)
    pstep, pcnt = pat[0]
    if pstep == 0:
        return (name, 0, 128, 0, 1 << 40)
    p0 = off // pstep
    f0 = off - p0 * pstep
    f1 = f0
    for st, cnt in pat[1:]:
        f1 += st * (cnt - 1)
    es = mybir.dt.size(ap.dtype)
    return (name, p0, p0 + pcnt, f0 * es, (f1 + 1) * es)


def _overlap(a, b):
    return a[1] < b[2] and b[1] < a[2] and a[3] < b[4] and b[3] < a[4]


def _covers(big, small):
    return big[1] <= small[1] and big[2] >= small[2] and big[3] <= small[3] and big[4] >= small[4]


class Op:
    __slots__ = ("eng", "fn", "waits", "need_inc", "dma_key", "stream", "sidx")


class Prog:
    ENGS = ("pe", "act", "dve", "pool", "sp")

    def __init__(self, nc):
        self.nc = nc
        self.q = {e: [] for e in self.ENGS}
        self.stream_ops = {}
        self.known = {e: {} for e in self.ENGS}
        self.ckpt = {}
        self.tw = {}
        self.tr = {}
        self.kver = {e: 0 for e in self.ENGS}
        self.ckver = {}

    def _add(self, queue, fn, reads, writes, dma_key=None):
        op = Op()
        op.eng = queue
        op.fn = fn
        op.dma_key = dma_key
        op.need_inc = dma_key is not None
        stream = dma_key if dma_key is not None else queue
        op.stream = stream
        lst = self.stream_ops.setdefault(stream, [])
        op.sidx = len(lst)
        lst.append(op)
        deps = {}

        def add_dep(o, kind):
            if o.stream == stream and dma_key is None:
                if queue == "pe":
                    return
            if deps.get(o.stream, -1) < o.sidx:
                deps[o.stream] = o.sidx

        rregs = [_region(a) for a in reads]
        wregs = [_region(a) for a in writes]
        if dma_key is not None and op.sidx > 0:
            deps[stream] = op.sidx - 1
        for r in rregs:
            for (wr, wo) in self.tw.get(r[0], ()):
                if _overlap(wr, r):
                    add_dep(wo, "raw")
            if len(r) > 5:
                for (rr, ro) in self.tr.get(r[0], ()):
                    if ro.stream != stream:
                        add_dep(ro, "rar")
        for w in wregs:
            for (wr, wo) in self.tw.get(w[0], ()):
                if _overlap(wr, w):
                    add_dep(wo, "waw")
            for (rr, ro) in self.tr.get(w[0], ()):
                if _overlap(rr, w):
                    add_dep(ro, "war")
        for w in wregs:
            tw = self.tw.setdefault(w[0], [])
            tw[:] = [x for x in tw if not _covers(w, x[0])]
            tw.append((w, op))
            tr = self.tr.get(w[0])
            if tr:
                tr[:] = [x for x in tr if not _covers(w, x[0])]
        for r in rregs:
            tr = self.tr.setdefault(r[0], [])
            tr[:] = [x for x in tr if not (x[1].stream == stream and _covers(r, x[0]))]
            tr.append((r, op))
        kn = self.known[queue]
        waits = [(s, i) for s, i in deps.items() if kn.get(s, -1) < i]
        for s, i in waits:
            self.stream_ops[s][i].need_inc = True
            if kn.get(s, -1) < i:
                kn[s] = i
            ck = self.ckpt.get(s)
            if ck:
                j = bisect.bisect_right(ck[0], i) - 1
                if j >= 0:
                    for s2, i2 in ck[1][j].items():
                        if kn.get(s2, -1) < i2:
                            kn[s2] = i2
        op.waits = waits
        if waits:
            self.kver[queue] += 1
        if self.ckver.get(stream) != self.kver[queue]:
            self.ckver[stream] = self.kver[queue]
            ck = self.ckpt.setdefault(stream, ([], []))
            snap = dict(kn)
            if dma_key is None:
                snap[stream] = op.sidx - 1
            ck[0].append(op.sidx)
            ck[1].append(snap)
        self.q[queue].append(op)
        return op

    def dma(self, key, out, in_, queue="sp"):
        return self._add(queue, lambda e: e.dma_start(out=out, in_=in_), [in_], [out], dma_key=key)

    def emit(self, final_waits=()):
        nc = self.nc
        sem_of = {}
        val_of = {}
        stack = ExitStack()
        nsem = 0
        for stream, ops in self.stream_ops.items():
            is_dma = ops[0].dma_key is not None
            ep_len = DMA_EPOCH if is_dma else EPOCH
            step = 16 if is_dma else 1
            cnt = 0
            for o in ops:
                if not o.need_inc:
                    continue
                ep = cnt // ep_len
                if (stream, ep) not in sem_of:
                    sem_of[(stream, ep)] = stack.enter_context(nc.semaphore("s%d" % nsem))
                    nsem += 1
                val_of[o] = (sem_of[(stream, ep)], (cnt % ep_len + 1) * step, step)
                cnt += 1
        self.nsem = nsem
        engines = {"pe": "tensor", "act": "scalar", "dve": "vector", "pool": "gpsimd", "sp": "sync"}
        fw = [val_of[o] for o in final_waits]
        with stack:
            with nc.Block() as block:
                for qn in self.ENGS:
                    ops = self.q[qn]
                    if not ops:
                        continue

                    def body(eng, ops=ops, qn=qn):
                        for o in ops:
                            for (s, i) in o.waits:
                                sem, val, _ = val_of[self.stream_ops[s][i]]
                                eng.wait_ge(sem, val)
                            inst = o.fn(eng)
                            if o.need_inc:
                                sem, val, step = val_of[o]
                                inst.then_inc(sem, step)
                        if qn == "sp":
                            for (sem, val, _) in fw:
                                eng.wait_ge(sem, val)

                    getattr(block, engines[qn])(body)


class Arena:
    def __init__(self, ap_bf, nbytes):
        self.ap = ap_bf
        self.cap = nbytes
        self.off = 0

    def get(self, shape, dt, parts=128):
        n = 1
        for s in shape:
            n *= s
        nb = n * mybir.dt.size(dt)
        a = self.ap[0:parts, self.off // 2:(self.off + nb) // 2]
        if dt != BF:
            a = a.bitcast(dt)
        if len(shape) == 2:
            a = a.rearrange("p (a b) -> p a b", b=shape[1])
        elif len(shape) == 3:
            a = a.rearrange("p (a b c) -> p a b c", b=shape[1], c=shape[2])
        self.off += (nb + 63) // 64 * 64
        assert self.off <= self.cap, (self.off, self.cap)
        return a


def build(S=4096, dbg=False, stop=None):
    NT = S // 128
    NST = S // 512
    nc = bass.Bass("TRN2", target_bir_lowering=False)
    es = ExitStack()
    es.enter_context(nc.allow_low_precision("bf16 matmul operands, fp32 accumulation"))

    def din(name, shape):
        return nc.dram_tensor(name, shape, F32, kind="ExternalInput").ap()

    x = din("x", [S, D])
    c_col = din("c_col", [128, 8])
    mod_w0 = din("mod_w0", [D, 3 * D])
    mod_w1 = din("mod_w1", [D, 3 * D])
    modb_col = din("modb_col", [128, 48])
    ng_col = din("ng_col", [128, 16])
    a_w_in = din("a_w_in", [D, 4 * W])
    alb_col = din("alb_col", [128, 32])
    aon_col = din("aon_col", [128, 16])
    a_w_out = din("a_w_out", [W, D])
    kv_mod_w = din("kv_mod_w", [D, 2 * D])
    kvmb_col = din("kvmb_col", [128, 16])
    kvng_col = din("kvng_col", [128, 8])
    kv_w = din("kv_w", [D, 2 * W + H])
    kv_fb = din("kv_fb", [H, 1])
    kng = din("kng", [128, 1])
    b_w_in = din("b_w_in", [D, 2 * W])
    qng = din("qng", [128, 1])
    b_w_out = din("b_w_out", [W, D])
    outp = nc.dram_tensor("out", [S, D], F32, kind="ExternalOutput").ap()
    dk = "ExternalOutput" if dbg else "Internal"
    uT = nc.dram_tensor("uT", [W, S], BF, kind="Internal").ap()
    x1 = nc.dram_tensor("x1", [S, D], F32, kind=dk).ap()
    KTs = nc.dram_tensor("KTs", [W, S], BF, kind="Internal").ap()
    Vs = nc.dram_tensor("Vs", [S, W], BF, kind="Internal").ap()

    P = Prog(nc)
    ARENA = 127 * 1024
    arena_t = es.enter_context(nc.sbuf_tensor("arena", [128, ARENA // 2], BF))
    hT = es.enter_context(nc.sbuf_tensor("hT", [128, 8, S], BF))
    cst = es.enter_context(nc.sbuf_tensor("cst", [128, 2560], F32))
    FT = es.enter_context(nc.sbuf_tensor("FT", [128, NT, 16], F32))
    Fd = nc.dram_tensor("Fd", [H, S], F32, kind="Internal").ap()
    PS = [es.enter_context(nc.psum_tensor("ps%d" % i, [128, 512], F32)) for i in range(8)]
    A = Arena(arena_t[:], ARENA)
    C = Arena(cst[:].bitcast(BF), 10240)

    def aps(*xs):
        return [a for a in xs if a is not None and not isinstance(a, (int, float))]

    def MM(out, lhsT, rhs, start=True, stop=True, skip=False):
        if skip:
            P._add("pe", lambda e: e.matmul(out, lhsT, rhs, start=start, stop=stop, skip_group_check=True),
                   [lhsT, rhs], [out])
        else:
            P._add("pe", lambda e: e.matmul(out, lhsT, rhs, start=start, stop=stop), [lhsT, rhs], [out])

    def TR(out, in_, ident):
        P._add("pe", lambda e: e.transpose(out, in_, ident), [in_, ident], [out])

    def ACT(out, in_, func, bias=0.0, scale=1.0, accum_out=None):
        P._add("act", lambda e: e.activation(out=out, in_=in_, func=func, bias=bias, scale=scale,
                                             accum_out=accum_out),
               aps(in_, bias, scale), aps(out, accum_out))

    def TT(out, in0, in1, op, eng="dve"):
        P._add(eng, lambda e: e.tensor_tensor(out=out, in0=in0, in1=in1, op=op), [in0, in1], [out])

    def TS(out, in0, s1, op0, s2=None, op1=None, eng="dve"):
        if op1 is None:
            P._add(eng, lambda e: e.tensor_scalar(out=out, in0=in0, scalar1=s1, scalar2=None, op0=op0),
                   aps(in0, s1), [out])
        else:
            P._add(eng, lambda e: e.tensor_scalar(out=out, in0=in0, scalar1=s1, scalar2=s2, op0=op0, op1=op1),
                   aps(in0, s1, s2), [out])

    def STT(out, in0, scalar, in1, op0, op1, eng="dve"):
        P._add(eng, lambda e: e.scalar_tensor_tensor(out=out, in0=in0, scalar=scalar, in1=in1, op0=op0, op1=op1),
               aps(in0, scalar, in1), [out])

    def COPY(out, in_, eng="dve"):
        if eng == "act":
            P._add("act", lambda e: e.copy(out=out, in_=in_), [in_], [out])
        else:
            P._add(eng, lambda e: e.tensor_copy(out=out, in_=in_), [in_], [out])

    def RECIP(out, in_):
        P._add("dve", lambda e: e.reciprocal(out=out, in_=in_), [in_], [out])

    def MEMSET(ap, v, eng="pool"):
        P._add(eng, lambda e: e.memset(ap, v), [], [ap])

    def ASEL(out, in_, pattern, op, fill, base, cm):
        P._add("pool", lambda e: e.affine_select(out=out, in_=in_, pattern=pattern, compare_op=op, fill=fill,
                                                 base=base, channel_multiplier=cm), [in_], [out])

    dcount = {}

    def DMA(cls, out, in_, nrot=1):
        k = dcount.get(cls, 0)
        dcount[cls] = k + 1
        return P.dma("%s%d" % (cls, k % nrot), out, in_)

    ident_f = C.get([128], F32)
    ident_b = C.get([128], BF)
    ones_b = C.get([128], BF)
    ones_f = C.get([128], F32)
    cmask = C.get([512], F32)
    amask = C.get([128], F32)
    cbias = C.get([128], F32)
    c_sb = C.get([8], F32)
    sc_sb = C.get([8], F32)
    modb = C.get([48], F32)
    ngc = C.get([16], F32)
    albc = C.get([32], F32)
    aonc = C.get([16], F32)
    kvmbc = C.get([16], F32)
    kvngc = C.get([8], F32)
    kngc = C.get([1], F32)
    qngc = C.get([1], F32)
    kvfbc = C.get([1], F32, parts=16)
    mod0 = C.get([24], F32)
    mod1 = C.get([24], F32)
    kvmod = C.get([16], F32)
    A0 = C.get([8], F32)
    A1 = C.get([8], F32)
    Akv = C.get([8], F32)
    lbc = C.get([16], F32)
    omlc = C.get([16], F32)
    nomlc = C.get([16], F32)
    qgs = C.get([1], F32)
    tmpc = C.get([32], F32)
    epsc = C.get([1], F32)
    ones512 = C.get([512], F32)

    MEMSET(epsc, EPS)
    MEMSET(ones512, 1.0)
    MEMSET(ident_f, 0.0)
    ASEL(ident_f, ident_f, [[-1, 128]], ALU.not_equal, 1.0, 0, 1)
    COPY(ident_b, ident_f)
    MEMSET(ones_b, 1.0)
    MEMSET(ones_f, 1.0)
    MEMSET(cmask, 1.0)
    MEMSET(cmask.rearrange("p (c t) -> p c t", t=64)[:, :, 0:1], 0.0)
    MEMSET(amask, 1.0)
    ASEL(amask, amask, [[1, 128]], ALU.is_ge, 0.0, 0, -1)
    MEMSET(amask[0:64, 64:128], 0.0)
    MEMSET(cbias, 0.0)
    ASEL(cbias, cbias, [[1, 128]], ALU.is_ge, NEG, 0, -1)
    for i, (dst, src) in enumerate([(c_sb, c_col), (modb, modb_col), (ngc, ng_col), (albc, alb_col),
                                    (aonc, aon_col), (kvmbc, kvmb_col), (kvngc, kvng_col), (kngc, kng),
                                    (qngc, qng), (kvfbc, kv_fb)]):
        P.dma("cst%d" % i, dst, src)

    ACT(sc_sb, c_sb, AF.Silu)

    def modcalc(wsrc, ncol, bcol, dst):
        A.off = 0
        nch = ncol // 128
        wt = A.get([8, ncol], F32)
        wv = wsrc.rearrange("(kc p) n -> p kc n", p=128)
        for kc in range(8):
            P.dma("mw%d" % (kc % 4), wt[:, kc, :], wv[:, kc, :])
        ps = PS[0]
        for m in range(nch):
            for kc in range(8):
                MM(ps[:, m:m + 1], wt[:, kc, m * 128:(m + 1) * 128], sc_sb[:, kc:kc + 1],
                   start=(kc == 0), stop=(kc == 7))
        TT(dst, ps[:, 0:nch], bcol, ALU.add)

    def modcalc_bg(wsrc, ncol, bcol, dst, off):
        wv = wsrc.rearrange("(kc p) n -> p kc n", p=128)
        save = A.off
        A.off = off
        regs = [A.get([8, 512], F32) for _ in range(2)]
        A.off = save
        ps = PS[0]
        for ci, c0 in enumerate(range(0, ncol, 512)):
            wt = regs[ci % 2]
            for half in range(2):
                P.dma("mwb%d" % ((2 * ci + half) % 4), wt[:, half * 4:(half + 1) * 4, :],
                      wv[:, half * 4:(half + 1) * 4, c0:c0 + 512], queue="pool")
            yield
            for m in range(4):
                mg = c0 // 128 + m
                for kc in range(8):
                    MM(ps[:, mg:mg + 1], wt[:, kc, m * 128:(m + 1) * 128], sc_sb[:, kc:kc + 1],
                       start=(kc == 0), stop=(kc == 7))
            yield
        TT(dst, ps[:, 0:ncol // 128], bcol, ALU.add)
        yield

    modcalc(mod_w0, 3 * D, modb[:, 0:24], mod0)
    STT(A0, mod0[:, 8:16], 1.0, ngc[:, 0:8], ALU.add, ALU.mult)
    TT(tmpc[:, 0:16], albc[:, 0:16], albc[:, 16:32], ALU.subtract)
    ACT(lbc, tmpc[:, 0:16], AF.Sigmoid)
    ACT(omlc, tmpc[:, 0:16], AF.Sigmoid, scale=-1.0)
    TS(nomlc, omlc, -1.0, ALU.mult)
    TS(qgs, qngc, 1.0 / math.sqrt(128.0), ALU.mult)

    def prepass_p1(xt, ss, xn):
        ACT(prepass_p1.junk, xt, AF.Square, accum_out=ss[:, 0:1])
        ACT(ss[:, 1:2], ss[:, 0:1], AF.Ln, bias=epsc, scale=1.0 / D)
        ACT(ss[:, 2:3], ss[:, 1:2], AF.Exp, scale=-0.5)
        TS(xn, xt, ss[:, 2:3], ALU.mult)

    def prepass_p2(xn, tt, Acol, shcol, pbank):
        for kc in range(8):
            pb = PS[pbank + kc // 4][:, 0:256].bitcast(BF)[:, (kc % 4) * 128:(kc % 4 + 1) * 128]
            TR(pb, xn[:, kc * 128:(kc + 1) * 128], ident_b)
        for kc in range(8):
            pb = PS[pbank + kc // 4][:, 0:256].bitcast(BF)[:, (kc % 4) * 128:(kc % 4 + 1) * 128]
            if kc < 4:
                ACT(hT[:, kc, tt * 128:(tt + 1) * 128], pb, AF.Identity, bias=shcol[:, kc:kc + 1],
                    scale=Acol[:, kc:kc + 1])
            else:
                TS(hT[:, kc, tt * 128:(tt + 1) * 128], pb, Acol[:, kc:kc + 1], ALU.mult,
                   shcol[:, kc:kc + 1], ALU.add)

    def prepass_from_dram(src, Acol, shcol, bg=None):
        A.off = 0
        xts = [A.get([D], F32) for _ in range(3)]
        prepass_p1.junk = A.get([D], F32)
        xns = [A.get([D], BF) for _ in range(2)]
        sss = [A.get([4], F32) for _ in range(2)]
        for tt in range(NT):
            xt = xts[tt % 3]
            DMA("xin", xt, src[tt * 128:(tt + 1) * 128, :], nrot=2)
            prepass_p1(xt, sss[tt % 2], xns[tt % 2])
            if tt > 0:
                prepass_p2(xns[(tt - 1) % 2], tt - 1, Acol, shcol, 4 + 2 * ((tt - 1) % 2))
            if bg is not None and next(bg, "done") == "done":
                bg = None
        prepass_p2(xns[(NT - 1) % 2], NT - 1, Acol, shcol, 4 + 2 * ((NT - 1) % 2))
        if bg is not None:
            for _ in bg:
                pass

    def load_w(src2d, c0, ncols, dst, stage):
        sv = src2d.rearrange("(kc p) n -> p kc n", p=128)
        for half in range(2):
            k = load_w.n
            load_w.n += 1
            st = stage[k % 2]
            P.dma("wst%d" % (k % 2), st[:, :, 0:ncols], sv[:, half * 4:(half + 1) * 4, c0:c0 + ncols])
            COPY(dst[:, half * 4:(half + 1) * 4, :], st[:, :, 0:ncols], eng="pool" if half == 0 else "act")
    load_w.n = 0

    def out_phase(w_out, xsrc, gcol, dst, next_pre=None):
        A.off = 0
        wo = A.get([16, D], BF)
        stage = [A.get([4, 512], F32) for _ in range(2)]
        wov = w_out.rearrange("(kc p) n -> p kc n", p=128)
        k = 0
        for q4 in range(4):
            for nh in range(2):
                st = stage[k % 2]
                P.dma("wst%d" % (k % 2), st, wov[:, q4 * 4:(q4 + 1) * 4, nh * 512:(nh + 1) * 512])
                COPY(wo[:, q4 * 4:(q4 + 1) * 4, nh * 512:(nh + 1) * 512], st, eng="pool")
                k += 1
        grow = A.get([D], F32)
        dg = A.get([128], F32)
        for kc in range(8):
            TS(dg, ident_f, gcol[:, kc:kc + 1], ALU.mult)
            pb = PS[0][:, 0:128]
            MM(pb, ones_f, dg)
            COPY(grow[:, kc * 128:(kc + 1) * 128], pb)
        uts = [A.get([16, 512], BF) for _ in range(2)]
        xts = [A.get([D], F32) for _ in range(2)]
        xos = [A.get([D], F32) for _ in range(3)]
        prepass_p1.junk = A.get([D], F32)
        xns = [A.get([D], BF) for _ in range(2)]
        sss = [A.get([4], F32) for _ in range(2)]
        uv = uT.rearrange("(kc p) s -> p kc s", p=128)
        outs = []
        for st_ in range(NST):
            ut = uts[st_ % 2]
            DMA("uin", ut, uv[:, :, st_ * 512:(st_ + 1) * 512], nrot=2)
            for t4 in range(4):
                tt = st_ * 4 + t4
                xt = xts[tt % 2]
                xo = xos[tt % 3]
                DMA("xin", xt, xsrc[tt * 128:(tt + 1) * 128, :], nrot=2)
                pb0 = 2 * (tt % 2)
                for nh in range(2):
                    for kc in range(16):
                        MM(PS[pb0 + nh][:, :], ut[:, kc, t4 * 128:(t4 + 1) * 128],
                           wo[:, kc, nh * 512:(nh + 1) * 512], start=(kc == 0), stop=(kc == 15))
                for nh in range(2):
                    sl = slice(nh * 512, (nh + 1) * 512)
                    TT(xo[:, sl], PS[pb0 + nh][:, :], grow[:, sl], ALU.mult)
                    TT(xo[:, sl], xo[:, sl], xt[:, sl], ALU.add, eng="pool")
                outs.append(DMA("xst", dst[tt * 128:(tt + 1) * 128, :], xo, nrot=4))
                if next_pre is not None:
                    prepass_p1(xo, sss[tt % 2], xns[tt % 2])
                    if tt > 0:
                        prepass_p2(xns[(tt - 1) % 2], tt - 1, next_pre[0], next_pre[1], 4 + 2 * ((tt - 1) % 2))
        if next_pre is not None:
            prepass_p2(xns[(NT - 1) % 2], NT - 1, next_pre[0], next_pre[1], 4 + 2 * ((NT - 1) % 2))
        return outs

    def _stop():
        P.emit()
        es.close()
        return nc, P

    if stop == "mod":
        return _stop()
    def _bg_mods():
        yield from modcalc_bg(kv_mod_w, 2 * D, kvmbc, kvmod, 24 * 1024)
        STT(Akv, kvmod[:, 8:16], 1.0, kvngc, ALU.add, ALU.mult)
        yield from modcalc_bg(mod_w1, 3 * D, modb[:, 24:48], mod1, 24 * 1024)
        STT(A1, mod1[:, 8:16], 1.0, ngc[:, 8:16], ALU.add, ALU.mult)

    prepass_from_dram(x, A0, mod0[:, 0:8], bg=_bg_mods())
    if stop == "pre":
        return _stop()

    A.off = 0
    stage = [A.get([4, 512], F32) for _ in range(2)]
    wq = A.get([8, 512], BF)
    wf = A.get([8, 512], BF)
    wi = A.get([8, 512], BF)
    wg = A.get([8, 512], BF)
    vg = A.get([NT, 512], BF)
    NB = 2
    T1 = [A.get([512], F32) for _ in range(NB)]
    T2 = [A.get([512], F32) for _ in range(NB)]
    T3 = [A.get([512], F32) for _ in range(NB)]
    T4 = [A.get([512], F32) for _ in range(NB)]
    T5 = [A.get([512], F32) for _ in range(NB)]
    qdec = [A.get([512], BF) for _ in range(NB)]
    kinb = [A.get([512], BF) for _ in range(NB)]
    kstb = [A.get([512], BF) for _ in range(NB)]
    sgbs = [A.get([512], BF) for _ in range(3)]
    poS = [A.get([512], F32) for _ in range(NB)]
    N1 = A.get([512], F32)
    N2 = A.get([512], F32)
    sqN = A.get([512], BF)
    sqb = [A.get([512], BF) for _ in range(NB)]
    ub = [A.get([512], BF) for _ in range(NB)]
    kstT = [A.get([4, 128], BF) for _ in range(NB)]
    attb = [A.get([128], BF) for _ in range(4)]
    S32 = [A.get([128], F32) for _ in range(2)]
    Sbf = [A.get([128], BF) for _ in range(2)]

    def stageA(h, j, st_, b_, s3):
        hs = slice(j * 128, (j + 1) * 128)
        ts_ = slice(st_ * 512, (st_ + 1) * 512)
        for (pb, wt) in ((PS[1], wf), (PS[2], wg), (PS[0], wq)):
            for kc in range(8):
                MM(pb[:, :], wt[:, kc, hs], hT[:, kc, ts_], start=(kc == 0), stop=(kc == 7))
            yield
        t1, t2, t3, t4_, t5 = T1[b_], T2[b_], T3[b_], T4[b_], T5[b_]
        ACT(t1, PS[1][:, :], AF.Sigmoid)
        ACT(sqb[b_], PS[2][:, :], AF.Sigmoid)
        yield
        TS(t2, t1, nomlc[:, h:h + 1], ALU.mult, omlc[:, h:h + 1], ALU.add, eng="pool")
        TT(sgbs[s3], PS[2][:, :], sqb[b_], ALU.mult)
        COPY(t5, PS[0][:, :], eng="act")
        yield
        ACT(t1, t1, AF.Ln, bias=lbc[:, h:h + 1], scale=omlc[:, h:h + 1])
        yield
        P._add("dve", lambda e, o=t3, d1=t1: e.tensor_tensor_scan(out=o, data0=cmask, data1=d1, initial=0.0,
                                                                  op0=ALU.mult, op1=ALU.add),
               [cmask, t1], [t3])
        yield
        ACT(t4_, t3, AF.Exp)
        ACT(t1, t3, AF.Exp, scale=-1.0)
        yield
        TT(qdec[b_], t5, t4_, ALU.mult, eng="pool")
        TT(t2, t2, t1, ALU.mult)
        yield
        COPY(kinb[b_], t2, eng="pool")
        eb3 = t4_.rearrange("p (c t) -> p c t", t=64)
        TT(kstb[b_].rearrange("p (c t) -> p c t", t=64), t2.rearrange("p (c t) -> p c t", t=64),
           eb3[:, :, 63:64].to_broadcast([128, 8, 64]), ALU.mult, eng="pool")
        yield
        ptr = PS[6][:, 0:256].bitcast(BF)
        for q4 in range(4):
            TR(ptr[:, q4 * 128:(q4 + 1) * 128], kstb[b_][:, q4 * 128:(q4 + 1) * 128], ident_b)
        COPY(kstT[b_].rearrange("p a b -> p (a b)"), ptr, eng="act")
        yield

    sidx_box = [0]

    def stageB(h, j, st_, b_):
        hs = slice(j * 128, (j + 1) * 128)
        ts_ = slice(st_ * 512, (st_ + 1) * 512)
        t1, t2, t4_ = T1[b_], T2[b_], T4[b_]
        po = PS[3]
        if st_ == 0:
            sidx_box[0] = 0
        for q4 in range(4):
            tt = st_ * 4 + q4
            cs = slice(q4 * 128, (q4 + 1) * 128)
            pa = PS[4][:, 0:128]
            MM(pa, kinb[b_][:, cs], qdec[b_][:, cs])
            ab = attb[q4]
            TT(ab, pa, amask, ALU.mult)
            for jj in range(2):
                sidx = sidx_box[0]
                n = st_ * 8 + q4 * 2 + jj
                rs = slice(jj * 64, (jj + 1) * 64)
                oc = slice(q4 * 128 + jj * 64, q4 * 128 + (jj + 1) * 64)
                first = (n == 0)
                pu = PS[5 if n % 2 == 0 else 7][:, 0:128]
                MM(pu, kstT[b_][rs, q4, :], vg[rs, tt, hs])
                MM(po[:, oc], vg[rs, tt, hs], ab[rs, rs], start=True, stop=first)
                if not first:
                    MM(po[:, oc], Sbf[sidx % 2], qdec[b_][:, oc], start=False, stop=True)
                dcol = t4_[:, q4 * 128 + jj * 64 + 63:q4 * 128 + jj * 64 + 64]
                if first:
                    COPY(Sbf[(sidx + 1) % 2], pu)
                    COPY(S32[(sidx + 1) % 2], pu)
                else:
                    STT(Sbf[(sidx + 1) % 2], S32[sidx % 2], dcol, pu, ALU.mult, ALU.add)
                    STT(S32[(sidx + 1) % 2], S32[sidx % 2], dcol, pu, ALU.mult, ALU.add)
                sidx_box[0] = sidx + 1
                yield
        COPY(poS[b_], po[:, :], eng="act")
        yield

    def stageC(h, j, st_, b_, s3):
        ts_ = slice(st_ * 512, (st_ + 1) * 512)
        ACT(sqN, poS[b_], AF.Square)
        yield
        yield
        MM(PS[6][:, :], ones_b, sqN)
        yield
        yield
        ACT(N1, PS[6][:, :], AF.Ln, bias=epsc, scale=1.0 / 128.0)
        yield
        ACT(N1, N1, AF.Exp, scale=-0.5)
        yield
        TT(N2, poS[b_], N1, ALU.mult)
        yield
        STT(ub[b_], N2, aonc[:, h:h + 1], sgbs[s3], ALU.mult, ALU.mult)
        DMA("ust", uT[h * 128:(h + 1) * 128, ts_], ub[b_], nrot=4)
        yield

    def interleave(ga, gb, gc=None, pre=3):
        gens = {"a": ga, "b": gb, "c": gc}
        if ga is not None:
            for _ in range(pre):
                if next(ga, "done") == "done":
                    gens["a"] = None
                    break
        while any(g is not None for g in gens.values()):
            for k in ("b", "c", "a"):
                g = gens[k]
                if g is not None and next(g, "done") == "done":
                    gens[k] = None

    hb = 0
    for g in range(4):
        load_w(a_w_in, 0 * W + g * 512, 512, wq, stage)
        load_w(a_w_in, 1 * W + g * 512, 512, wf, stage)
        load_w(a_w_in, 2 * W + g * 512, 512, wi, stage)
        load_w(a_w_in, 3 * W + g * 512, 512, wg, stage)
        for tt in range(NT):
            pv = PS[6 + tt % 2]
            for kc in range(8):
                MM(pv[:, :], hT[:, kc, tt * 128:(tt + 1) * 128], wi[:, kc, :], start=(kc == 0), stop=(kc == 7))
            if tt % 2 == 0:
                COPY(vg[:, tt, :], pv[:, :], eng="act")
            else:
                COPY(vg[:, tt, :], pv[:, :])
        tiles = [(g * 4 + j, j, st_) for j in range(4) for st_ in range(NST)]
        prev = None
        prev2 = None
        for (h, j, st_) in tiles:
            b_ = hb % NB
            s3 = hb % 3
            hb += 1
            ga = stageA(h, j, st_, b_, s3)
            gb = stageB(*prev[:4]) if prev is not None else None
            gc = stageC(*prev2) if prev2 is not None else None
            interleave(ga, gb, gc)
            prev2 = prev
            prev = (h, j, st_, b_, s3)
        interleave(None, stageB(*prev[:4]), stageC(*prev2) if prev2 is not None else None)
        interleave(None, None, stageC(*prev))

    if stop == "l0b":
        return _stop()
    out_phase(a_w_out, x, mod0[:, 16:24], x1, next_pre=(Akv, kvmod[:, 0:8]))

    if stop == "l0c":
        return _stop()
    A.off = 0
    stage = [A.get([4, 512], F32) for _ in range(2)]
    wk = A.get([8, 512], BF)
    wv = A.get([8, 512], BF)
    wfl = A.get([8, 16], BF)
    vst = [A.get([4, 512], BF) for _ in range(2)]
    T1 = [A.get([512], F32) for _ in range(2)]
    sqb = [A.get([512], BF) for _ in range(2)]
    ktb = [A.get([512], BF) for _ in range(2)]
    Fall = A.get([S], F32, parts=16)
    lsg = A.get([512], F32, parts=16)

    sv = kv_w.rearrange("(kc p) n -> p kc n", p=128)
    P.dma("wst0", stage[0][:, :, 0:16], sv[:, 0:4, 2 * W:2 * W + 16])
    COPY(wfl[:, 0:4, :], stage[0][:, :, 0:16], eng="pool")
    P.dma("wst1", stage[1][:, :, 0:16], sv[:, 4:8, 2 * W:2 * W + 16])
    COPY(wfl[:, 4:8, :], stage[1][:, :, 0:16], eng="pool")
    for st_ in range(NST):
        ts_ = slice(st_ * 512, (st_ + 1) * 512)
        pf = PS[4][0:16, :]
        for kc in range(8):
            MM(pf, wfl[:, kc, :], hT[:, kc, ts_], start=(kc == 0), stop=(kc == 7))
        ACT(lsg, pf, AF.Sigmoid, bias=kvfbc)
        ACT(lsg, lsg, AF.Ln)
        init = 0.0 if st_ == 0 else Fall[:, st_ * 512 - 1:st_ * 512]
        P._add("dve", lambda e, o=Fall[:, ts_], ini=init: e.tensor_tensor_scan(
            out=o, data0=ones512[0:16, :], data1=lsg, initial=ini, op0=ALU.mult, op1=ALU.add),
            aps(lsg, init, ones512[0:16, :]), [Fall[:, ts_]])
    P.dma("fst0", Fd, Fall)
    for kb in range(NT):
        pt = PS[5][:, 0:16]
        TR(pt, Fall[:, kb * 128:(kb + 1) * 128], ident_f[0:16, 0:16])
        COPY(FT[:, kb, :], pt)

    for g in range(4):
        load_w(kv_w, g * 512, 512, wk, stage)
        load_w(kv_w, W + g * 512, 512, wv, stage)
        for tt in range(NT):
            pv = PS[2 + tt % 2]
            for kc in range(8):
                MM(pv[:, :], hT[:, kc, tt * 128:(tt + 1) * 128], wv[:, kc, :], start=(kc == 0), stop=(kc == 7))
            vs_ = vst[(tt // 4) % 2]
            if tt % 2 == 0:
                COPY(vs_[:, tt % 4, :], pv[:, :], eng="act")
            else:
                COPY(vs_[:, tt % 4, :], pv[:, :])
            if tt % 4 == 3:
                t0 = tt - 3
                DMA("vst", Vs[t0 * 128:(t0 + 4) * 128, g * 512:(g + 1) * 512].rearrange("(a p) n -> p a n", p=128),
                    vs_, nrot=2)
        for j in range(4):
            h = g * 4 + j
            hs = slice(j * 128, (j + 1) * 128)
            for st_ in range(NST):
                b_ = st_ % 2
                ts_ = slice(st_ * 512, (st_ + 1) * 512)
                pk = PS[0 if b_ == 0 else 5]
                pss = PS[1 if b_ == 0 else 6]
                for kc in range(8):
                    MM(pk[:, :], wk[:, kc, hs], hT[:, kc, ts_], start=(kc == 0), stop=(kc == 7))
                ACT(sqb[b_], pk[:, :], AF.Square)
                MM(pss[:, :], ones_b, sqb[b_])
                ACT(T1[b_], pss[:, :], AF.Ln, bias=epsc, scale=1.0 / 128.0)
                ACT(T1[b_], T1[b_], AF.Exp, scale=-0.5)
                STT(ktb[b_], pk[:, :], kngc[:, 0:1], T1[b_], ALU.mult, ALU.mult)
                DMA("kst", KTs[h * 128:(h + 1) * 128, ts_], ktb[b_], nrot=2)

    if stop == "kv":
        return _stop()
    prepass_from_dram(x1, A1, mod1[:, 0:8])

    A.off = 0
    stage = [A.get([4, 512], F32) for _ in range(2)]
    wq = A.get([8, 512], BF)
    wg = A.get([8, 512], BF)
    KT = A.get([S], BF)
    Vaug = A.get([NT, 130], BF)
    QT = A.get([S], BF)
    sgT = A.get([S], BF)
    Fb = A.get([S], F32)
    nFc = A.get([NT], F32)
    TMP = [A.get([512], F32) for _ in range(6)]
    PT = [A.get([512], BF) for _ in range(6)]
    T1 = [A.get([512], F32) for _ in range(2)]
    sqb = [A.get([512], BF) for _ in range(2)]
    E1 = [A.get([512], F32) for _ in range(2)]
    otm4 = [A.get([128], BF) for _ in range(4)]
    rden4 = [A.get([1], F32) for _ in range(4)]
    accS = [[A.get([385], F32) for _ in range(2)] for _ in range(2)]
    epi_q = []
    epi_n = [0]
    ub = [A.get([512], BF) for _ in range(2)]
    MEMSET(Vaug[:, :, 128:129], 1.0)
    it = 0
    oi = 0
    for g in range(4):
        load_w(b_w_in, g * 512, 512, wq, stage)
        load_w(b_w_in, W + g * 512, 512, wg, stage)
        for j4 in range(4):
            h = g * 4 + j4
            hs = slice(j4 * 128, (j4 + 1) * 128)
            P.dma("kin0", KT, KTs[h * 128:(h + 1) * 128, :])
            vv = Vs[:, h * 128:(h + 1) * 128].rearrange("(kb p) n -> p kb n", p=128)
            hv = NT // 2
            P.dma("vin0", Vaug[:, 0:hv, 0:128], vv[:, 0:hv, :])
            P.dma("vin1", Vaug[:, hv:NT, 0:128], vv[:, hv:NT, :])
            P.dma("fin0", Fb, Fd[h:h + 1, :].to_broadcast([128, S]))
            TS(nFc, FT[:, :, h], -1.0, ALU.mult)
            for st_ in range(NST):
                b_ = st_ % 2
                ts_ = slice(st_ * 512, (st_ + 1) * 512)
                pq = PS[0 if b_ == 0 else 4]
                pg = PS[1 if b_ == 0 else 5]
                pss = PS[2 if b_ == 0 else 3]
                for kc in range(8):
                    MM(pq[:, :], wq[:, kc, hs], hT[:, kc, ts_], start=(kc == 0), stop=(kc == 7))
                for kc in range(8):
                    MM(pg[:, :], wg[:, kc, hs], hT[:, kc, ts_], start=(kc == 0), stop=(kc == 7))
                ACT(sqb[b_], pq[:, :], AF.Square)
                MM(pss[:, :], ones_b, sqb[b_])
                ACT(T1[b_], pss[:, :], AF.Ln, bias=epsc, scale=1.0 / 128.0)
                ACT(T1[b_], T1[b_], AF.Exp, scale=-0.5)
                STT(QT[:, ts_], pq[:, :], qgs[:, 0:1], T1[b_], ALU.mult, ALU.mult)
                ACT(E1[b_], pg[:, :], AF.Exp, scale=-1.0)
                ACT(E1[b_], E1[b_], AF.Ln, bias=ones_f[:, 0:1])
                ACT(E1[b_], E1[b_], AF.Exp, scale=-1.0)
                TT(sgT[:, ts_], pg[:, :], E1[b_], ALU.mult)
            items = [(qs, kb) for qs in range(NST) for kb in range(4 * qs + 4)]
            accs = [PS[6 + jq // 2][:, (jq % 2) * 256:(jq % 2) * 256 + 129] for jq in range(4)]
            LOOK = 4
            STB = (4, 5, 0, 1)
            pend = {}

            def front(qs, kb):
                nonlocal it
                j0 = max(0, kb - 4 * qs)
                n = 512 - 128 * j0
                qcols = slice(qs * 512 + j0 * 128, (qs + 1) * 512)
                b2 = it % 4
                b3 = it % 6
                it += 1
                pst = PS[STB[b2]][:, 0:n]
                MM(pst, KT[:, kb * 128:(kb + 1) * 128], QT[:, qcols])
                tmp = TMP[b3][:, 0:n]
                TT(tmp, pst, Fb[:, qcols], ALU.add)
                if kb >= 4 * qs:
                    TT(tmp[:, 0:128], tmp[:, 0:128], cbias, ALU.add)
                pt_ = PT[b3][:, 0:n]
                ACT(pt_, tmp, AF.Exp, bias=nFc[:, kb:kb + 1])
                pend[(qs, kb)] = (pt_, j0)

            def back(qs, kb):
                nonlocal oi
                pt_, j0 = pend.pop((qs, kb))
                for jq in range(j0, 4):
                    MM(accs[jq], pt_[:, (jq - j0) * 128:(jq - j0 + 1) * 128], Vaug[:, kb, 0:129],
                       start=(kb == 0 and jq % 2 == 0), stop=(kb == 4 * qs + jq), skip=True)
                if kb == 4 * qs + 3:
                    sl = epi_n[0] % 2
                    epi_n[0] += 1
                    for bnk in range(2):
                        COPY(accS[sl][bnk], PS[6 + bnk][:, 0:385])
                    epi_q.append([1, qs, sl, 0])

            def epi2(qs, sl):
                for jq in range(4):
                    src = accS[sl][jq // 2]
                    c0 = (jq % 2) * 256
                    RECIP(rden4[jq], src[:, c0 + 128:c0 + 129])
                    TS(otm4[jq], src[:, c0:c0 + 128], rden4[jq][:, 0:1], ALU.mult)

            def epi3(qs, sl):
                ub_ = ub[qs % 2]
                for jq in range(4):
                    ptr = PS[3][:, 0:256].bitcast(BF)[:, jq * 128:(jq + 1) * 128]
                    TR(ptr, otm4[jq], ident_b)
                for jq in range(4):
                    ptr = PS[3][:, 0:256].bitcast(BF)[:, jq * 128:(jq + 1) * 128]
                    TT(ub_[:, jq * 128:(jq + 1) * 128], ptr,
                       sgT[:, qs * 512 + jq * 128:qs * 512 + (jq + 1) * 128], ALU.mult)
                DMA("ust", uT[h * 128:(h + 1) * 128, qs * 512:(qs + 1) * 512], ub_, nrot=4)

            def epi_step(flush=False):
                for e in list(epi_q):
                    e[3] += 1
                    if e[0] == 1 and (flush or e[3] >= 2):
                        epi2(e[1], e[2])
                        e[0] = 2
                        if not flush:
                            continue
                    if e[0] == 2 and (flush or e[3] >= 4):
                        epi3(e[1], e[2])
                        epi_q.remove(e)

            for idx in range(len(items) + LOOK):
                if idx < len(items):
                    front(*items[idx])
                if idx - LOOK >= 0:
                    back(*items[idx - LOOK])
                epi_step()
            epi_step(flush=True)

    outs = out_phase(b_w_out, x1, mod1[:, 16:24], outp)
    P.emit(final_waits=outs)
    es.close()
    return nc, P


def _prep_inputs(inp, b, S):
    f = lambda a: np.ascontiguousarray(a, dtype=np.float32)
    return {
        "x": f(inp["x"][b, :S]),
        "c_col": f(inp["c"][b].reshape(8, 128).T),
        "mod_w0": f(inp["mod_w"][0]),
        "mod_w1": f(inp["mod_w"][1]),
        "modb_col": f(inp["mod_b"].reshape(2, 24, 128).transpose(2, 0, 1).reshape(128, 48)),
        "ng_col": f(inp["norm_g"].reshape(2, 8, 128).transpose(2, 0, 1).reshape(128, 16)),
        "a_w_in": f(inp["a_w_in"][0]),
        "alb_col": f(inp["a_lb_logits"].reshape(2, 16, 128).transpose(2, 0, 1).reshape(128, 32)),
        "aon_col": f(inp["a_onorm_g"][0].reshape(16, 128).T),
        "a_w_out": f(inp["a_w_out"][0]),
        "kv_mod_w": f(inp["kv_mod_w"]),
        "kvmb_col": f(inp["kv_mod_b"].reshape(16, 128).T),
        "kvng_col": f(inp["kv_norm_g"].reshape(8, 128).T),
        "kv_w": f(inp["kv_w"]),
        "kv_fb": f(inp["kv_fb"].reshape(16, 1)),
        "kng": f(inp["k_norm_g"].reshape(128, 1)),
        "b_w_in": f(inp["b_w_in"][0]),
        "qng": f(inp["b_q_norm_g"][0].reshape(128, 1)),
        "b_w_out": f(inp["b_w_out"][0]),
    }


_CACHE = {}


def kernel(**inputs):
    inp = {k: np.asarray(v) for k, v in inputs.items()}
    S = inp["x"].shape[1]
    if S not in _CACHE:
        _CACHE[S] = build(S)[0]
    nc = _CACHE[S]
    in_maps = [_prep_inputs(inp, b, S) for b in range(8)]
    res = run_bass_kernel_spmd(nc, in_maps, core_ids=list(range(8)))
    return np.stack([np.asarray(r["out"], dtype=np.float32) for r in res.results], axis=0)
```
